# Optimizing a Trainium2 kernel written in Bass

```python
import math
import jax, jax.numpy as jnp
from jax import lax
import numpy as np

D_MODEL = 1024
BATCH = 16
SEQ = 2048
DEPTH = 4

GRID_W = 64
CTX_LEN = 256
N_MIXERS = 3
Q_BLOCK = 128
ROPE_THETA = 10000.0
LN_EPS = 1e-5
RMS_EPS = 1e-6
D_FF = 4 * D_MODEL

MLA_HEADS = 8
MLA_Q_RANK = 512
MLA_KV_RANK = 256
MLA_NOPE = 128
MLA_ROPE = 64
MLA_V = 128

NA_HEADS = 16
NA_HEAD_DIM = D_MODEL // NA_HEADS
NA_WIN_ROWS = 8
NA_WIN_COLS = 16

GQA_HEAD_DIM = 128
GQA_Q_HEADS = D_MODEL // GQA_HEAD_DIM
GQA_KV_HEADS = GQA_Q_HEADS // 4
GQA_GROUP = GQA_Q_HEADS // GQA_KV_HEADS

kernel_name = "hybrid_mla_na_gqa_dit_trunk"


def _layer_norm(x, g, b):
    xf = x.astype(jnp.float32)
    mu = jnp.mean(xf, -1, keepdims=True)
    var = jnp.mean(jnp.square(xf - mu), -1, keepdims=True)
    return ((xf - mu) * lax.rsqrt(var + LN_EPS) * g + b).astype(x.dtype)


def _rms_norm(x, g):
    xf = x.astype(jnp.float32)
    return (xf * lax.rsqrt(jnp.mean(xf * xf, -1, keepdims=True) + RMS_EPS) * g).astype(x.dtype)


def _rope_1d(x, pos):
    half = x.shape[-1] // 2
    freqs = ROPE_THETA ** (-jnp.arange(half, dtype=jnp.float32) / half)
    ang = pos.astype(jnp.float32)[:, None, None] * freqs
    cos, sin = jnp.cos(ang), jnp.sin(ang)
    xf = x.astype(jnp.float32)
    x1, x2 = xf[..., :half], xf[..., half:]
    return jnp.concatenate([x1 * cos - x2 * sin, x1 * sin + x2 * cos], -1).astype(x.dtype)


def _rope_2d(x, rows, cols):
    h = x.shape[-1] // 2
    return jnp.concatenate([_rope_1d(x[..., :h], rows), _rope_1d(x[..., h:], cols)], -1)


def _grid_positions(s):
    t = jnp.arange(s, dtype=jnp.int32)
    return t // GRID_W, t % GRID_W


def _block_attention(q, k, v, scale):
    b, sq, hk, g, dk = q.shape
    nb = sq // Q_BLOCK
    qb = q.reshape(b, nb, Q_BLOCK, hk, g, dk).transpose(1, 0, 2, 3, 4, 5)

    def step(qi):
        s = jnp.einsum('bqhgd,bkhd->bhgqk', qi, k).astype(jnp.float32) * scale
        p = jax.nn.softmax(s, axis=-1).astype(v.dtype)
        return jnp.einsum('bhgqk,bkhd->bqhgd', p, v)

    o = lax.map(step, qb)
    return o.transpose(1, 0, 2, 3, 4, 5).reshape(b, sq, hk * g * v.shape[-1])


def _mla_q(h, w_dq, q_norm, w_uq, rows, cols):
    b, s, _ = h.shape
    q = (_rms_norm(h @ w_dq, q_norm) @ w_uq).reshape(b, s, MLA_HEADS, MLA_NOPE + MLA_ROPE)
    q_nope, q_pe = q[..., :MLA_NOPE], q[..., MLA_NOPE:]
    if rows is not None:
        q_pe = _rope_2d(q_pe, rows, cols)
    return jnp.concatenate([q_nope, q_pe], -1)


def _mla_kv(h, w_dkv, kv_norm, w_ukv, rows, cols):
    b, s, _ = h.shape
    ckv = h @ w_dkv
    c_kv, k_pe = ckv[..., :MLA_KV_RANK], ckv[..., MLA_KV_RANK:]
    kv = (_rms_norm(c_kv, kv_norm) @ w_ukv).reshape(b, s, MLA_HEADS, MLA_NOPE + MLA_V)
    k_nope, v = kv[..., :MLA_NOPE], kv[..., MLA_NOPE:]
    k_pe = k_pe[:, :, None, :]
    if rows is not None:
        k_pe = _rope_2d(k_pe, rows, cols)
    k = jnp.concatenate([k_nope, jnp.broadcast_to(k_pe, (b, s, MLA_HEADS, MLA_ROPE))], -1)
    return k, v


def _mla_mixer(h, hc, w_dq, q_norm, w_uq, w_dkv, kv_norm, w_ukv, w_o, need_ctx):
    rows, cols = _grid_positions(h.shape[1])
    scale = (MLA_NOPE + MLA_ROPE) ** -0.5
    q = _mla_q(h, w_dq, q_norm, w_uq, rows, cols)
    k, v = _mla_kv(h, w_dkv, kv_norm, w_ukv, rows, cols)
    kc, vc = _mla_kv(hc, w_dkv, kv_norm, w_ukv, None, None)
    k_all = jnp.concatenate([kc, k], 1)
    v_all = jnp.concatenate([vc, v], 1)
    out = _block_attention(q[:, :, :, None], k_all, v_all, scale) @ w_o
    out_c = None
    if need_ctx:
        qc = _mla_q(hc, w_dq, q_norm, w_uq, None, None)
        out_c = _block_attention(qc[:, :, :, None], kc, vc, scale) @ w_o
    return out, out_c


def _na_mixer(h, hc, w_qkv, b_qkv, rpb, w_o, need_ctx):
    b, s, _ = h.shape
    n_rows = s // GRID_W
    kr = min(NA_WIN_ROWS, n_rows)
    scale = NA_HEAD_DIM ** -0.5

    def proj(t):
        qkv = (t @ w_qkv + b_qkv).reshape(t.shape[0], t.shape[1], 3, NA_HEADS, NA_HEAD_DIM)
        return qkv[:, :, 0], qkv[:, :, 1], qkv[:, :, 2]

    q, k, v = proj(h)
    qc, kc, vc = proj(hc)
    grid = (b, n_rows, GRID_W, NA_HEADS, NA_HEAD_DIM)
    q_grid, k_grid, v_grid = q.reshape(grid), k.reshape(grid), v.reshape(grid)

    r_idx = jnp.arange(n_rows, dtype=jnp.int32)
    row_start = jnp.clip(r_idx - kr // 2, 0, n_rows - kr)
    c_idx = jnp.arange(GRID_W, dtype=jnp.int32)
    col_start = jnp.clip(c_idx - NA_WIN_COLS // 2, 0, GRID_W - NA_WIN_COLS)
    col_valid = (c_idx[None, :] >= col_start[:, None]) & (c_idx[None, :] < col_start[:, None] + NA_WIN_COLS)
    mask = jnp.tile(col_valid, (1, kr))
    dc = jnp.clip(c_idx[None, :] - c_idx[:, None] + NA_WIN_COLS - 1, 0, 2 * NA_WIN_COLS - 2)
    rpb_cols = rpb[:, :, dc]

    def row_step(args):
        r, rs = args
        q_row = lax.dynamic_index_in_dim(q_grid, r, axis=1, keepdims=False)
        k_strip = lax.dynamic_slice_in_dim(k_grid, rs, kr, axis=1).reshape(b, kr * GRID_W, NA_HEADS, NA_HEAD_DIM)
        v_strip = lax.dynamic_slice_in_dim(v_grid, rs, kr, axis=1).reshape(b, kr * GRID_W, NA_HEADS, NA_HEAD_DIM)
        dr = rs + jnp.arange(kr, dtype=jnp.int32) - r + NA_WIN_ROWS - 1
        bias = jnp.take(rpb_cols, dr, axis=1).transpose(0, 2, 1, 3).reshape(NA_HEADS, GRID_W, kr * GRID_W)
        s_lat = jnp.einsum('bqhd,bkhd->bhqk', q_row, k_strip).astype(jnp.float32) * scale + bias
        s_lat = jnp.where(mask, s_lat, -jnp.inf)
        s_ctx = jnp.einsum('bqhd,bkhd->bhqk', q_row, kc).astype(jnp.float32) * scale
        p = jax.nn.softmax(jnp.concatenate([s_ctx, s_lat], -1), axis=-1).astype(v.dtype)
        return jnp.einsum('bhqk,bkhd->bqhd', p, jnp.concatenate([vc, v_strip], 1))

    o = lax.map(row_step, (r_idx, row_start))
    out = o.transpose(1, 0, 2, 3, 4).reshape(b, s, NA_HEADS * NA_HEAD_DIM) @ w_o
    out_c = None
    if need_ctx:
        out_c = _block_attention(qc[:, :, :, None], kc, vc, scale) @ w_o
    return out, out_c


def _gqa_proj(t, w_qkv, q_norm, k_norm, rows, cols):
    bb, ss, _ = t.shape
    nq = GQA_Q_HEADS * GQA_HEAD_DIM
    nk = GQA_KV_HEADS * GQA_HEAD_DIM
    qkv = t @ w_qkv
    q = _rms_norm(qkv[..., :nq].reshape(bb, ss, GQA_Q_HEADS, GQA_HEAD_DIM), q_norm)
    k = _rms_norm(qkv[..., nq:nq + nk].reshape(bb, ss, GQA_KV_HEADS, GQA_HEAD_DIM), k_norm)
    v = qkv[..., nq + nk:].reshape(bb, ss, GQA_KV_HEADS, GQA_HEAD_DIM)
    if rows is not None:
        q = _rope_2d(q, rows, cols)
        k = _rope_2d(k, rows, cols)
    return q.reshape(bb, ss, GQA_KV_HEADS, GQA_GROUP, GQA_HEAD_DIM), k, v


def _gqa_mixer(h, hc, w_qkv, q_norm, k_norm, w_o, need_ctx):
    rows, cols = _grid_positions(h.shape[1])
    scale = GQA_HEAD_DIM ** -0.5
    q, k, v = _gqa_proj(h, w_qkv, q_norm, k_norm, rows, cols)
    qc, kc, vc = _gqa_proj(hc, w_qkv, q_norm, k_norm, None, None)
    out = _block_attention(q, jnp.concatenate([kc, k], 1), jnp.concatenate([vc, v], 1), scale) @ w_o
    out_c = None
    if need_ctx:
        out_c = _block_attention(qc, kc, vc, scale) @ w_o
    return out, out_c


def _mlp(h, w1, w2):
    return jnp.square(jax.nn.relu(h @ w1)) @ w2


def setup_inputs(seed: int = 0) -> dict:
    key = jax.random.key(seed)
    ks = iter(jax.random.split(key, 32))
    beta = (8.0 * DEPTH) ** -0.25
    n_a = len(range(0, DEPTH, N_MIXERS))
    n_b = len(range(1, DEPTH, N_MIXERS))
    n_c = len(range(2, DEPTH, N_MIXERS))
    D = D_MODEL

    def nrm(shape, s=1.0):
        return jax.random.normal(next(ks), shape, jnp.float32) * s

    def w(shape, fan_in, g=1.0):
        return nrm(shape, g * fan_in ** -0.5)

    def gain(shape):
        return 1.0 + nrm(shape, 0.05)

    return {
        "x": nrm((BATCH, SEQ, D)),
        "c": nrm((BATCH, D)),
        "ctx": nrm((BATCH, CTX_LEN, D)),
        "c_ctx": nrm((D,)),
        "ada_w": w((DEPTH, D, 6 * D), D),
        "ada_b": nrm((DEPTH, 6 * D), 0.02),
        "ln1_g": gain((DEPTH, D)),
        "ln1_b": nrm((DEPTH, D), 0.02),
        "ln2_g": gain((DEPTH, D)),
        "ln2_b": nrm((DEPTH, D), 0.02),
        "mlp_w1": w((DEPTH, D, D_FF), D),
        "mlp_w2": w((DEPTH, D_FF, D), D_FF, beta),
        "mla_w_dq": w((n_a, D, MLA_Q_RANK), D),
        "mla_q_norm": gain((n_a, MLA_Q_RANK)),
        "mla_w_uq": w((n_a, MLA_Q_RANK, MLA_HEADS * (MLA_NOPE + MLA_ROPE)), MLA_Q_RANK),
        "mla_w_dkv": w((n_a, D, MLA_KV_RANK + MLA_ROPE), D),
        "mla_kv_norm": gain((n_a, MLA_KV_RANK)),
        "mla_w_ukv": w((n_a, MLA_KV_RANK, MLA_HEADS * (MLA_NOPE + MLA_V)), MLA_KV_RANK),
        "mla_w_o": w((n_a, MLA_HEADS * MLA_V, D), MLA_HEADS * MLA_V, beta),
        "na_w_qkv": w((n_b, D, 3 * NA_HEADS * NA_HEAD_DIM), D),
        "na_b_qkv": nrm((n_b, 3 * NA_HEADS * NA_HEAD_DIM), 0.02),
        "na_rpb": nrm((n_b, NA_HEADS, 2 * NA_WIN_ROWS - 1, 2 * NA_WIN_COLS - 1), 0.1),
        "na_w_o": w((n_b, NA_HEADS * NA_HEAD_DIM, D), NA_HEADS * NA_HEAD_DIM, beta),
        "gqa_w_qkv": w((n_c, D, (GQA_Q_HEADS + 2 * GQA_KV_HEADS) * GQA_HEAD_DIM), D),
        "gqa_q_norm": gain((n_c, GQA_HEAD_DIM)),
        "gqa_k_norm": gain((n_c, GQA_HEAD_DIM)),
        "gqa_w_o": w((n_c, GQA_Q_HEADS * GQA_HEAD_DIM, D), GQA_Q_HEADS * GQA_HEAD_DIM, beta),
    }


def reference(x, c, ctx, c_ctx, ada_w, ada_b, ln1_g, ln1_b, ln2_g, ln2_b, mlp_w1, mlp_w2,
              mla_w_dq, mla_q_norm, mla_w_uq, mla_w_dkv, mla_kv_norm, mla_w_ukv, mla_w_o,
              na_w_qkv, na_b_qkv, na_rpb, na_w_o,
              gqa_w_qkv, gqa_q_norm, gqa_k_norm, gqa_w_o):
    alpha = (2.0 * DEPTH) ** 0.25
    for i in range(DEPTH):
        kind, j = i % N_MIXERS, i // N_MIXERS
        need_ctx = i < DEPTH - 1
        mod = (jax.nn.silu(c) @ ada_w[i] + ada_b[i])[:, None, :]
        mod_c = jax.nn.silu(c_ctx) @ ada_w[i] + ada_b[i]
        sh1, sc1, g1, sh2, sc2, g2 = jnp.split(mod, 6, axis=-1)
        csh1, csc1, cg1, csh2, csc2, cg2 = jnp.split(mod_c, 6, axis=-1)

        h = x * (1 + sc1) + sh1
        hc = ctx * (1 + csc1) + csh1
        if kind == 0:
            y, yc = _mla_mixer(h, hc, mla_w_dq[j], mla_q_norm[j], mla_w_uq[j], mla_w_dkv[j],
                               mla_kv_norm[j], mla_w_ukv[j], mla_w_o[j], need_ctx)
        elif kind == 1:
            y, yc = _na_mixer(h, hc, na_w_qkv[j], na_b_qkv[j], na_rpb[j], na_w_o[j], need_ctx)
        else:
            y, yc = _gqa_mixer(h, hc, gqa_w_qkv[j], gqa_q_norm[j], gqa_k_norm[j], gqa_w_o[j], need_ctx)
        x = _layer_norm(alpha * x + g1 * y, ln1_g[i], ln1_b[i])
        if need_ctx:
            ctx = _layer_norm(alpha * ctx + cg1 * yc, ln1_g[i], ln1_b[i])

        x = _layer_norm(alpha * x + g2 * _mlp(x * (1 + sc2) + sh2, mlp_w1[i], mlp_w2[i]), ln2_g[i], ln2_b[i])
        if need_ctx:
            ctx = _layer_norm(alpha * ctx + cg2 * _mlp(ctx * (1 + csc2) + csh2, mlp_w1[i], mlp_w2[i]),
                              ln2_g[i], ln2_b[i])
    return x
```

```python
import math
from contextlib import ExitStack
import numpy as np
import concourse.bass as bass
import concourse.mybir as mybir
from concourse.bass_utils import run_bass_kernel_spmd

F32 = mybir.dt.float32
BF16 = mybir.dt.bfloat16
AF = mybir.ActivationFunctionType
ALU = mybir.AluOpType

D = 1024
L = 2048
C = 256
T = L + C
DEPTH = 4
ALPHA = (2.0 * DEPTH) ** 0.25
LN_EPS = 1e-5 / (ALPHA * ALPHA)
RMS_EPS = 1e-6
TCH = [(0, 512, False), (512, 512, False), (1024, 512, False), (1536, 512, False), (2048, 256, True)]
BLK = 256


def _esize(dt):
    return 2 if dt == BF16 else 4


class Sched:
    ENG = ["pe", "act", "dve", "pool", "sp"]

    def __init__(self):
        self.ops = []
        self.lastw = {}
        self.readers = {}

    @staticmethod
    def keys(item):
        if isinstance(item, (str, tuple)):
            return [item]
        ap = item
        name = ap.tensor.name
        if str(ap.space) == "DRAM":
            return [name]
        dims = ap.ap
        es = _esize(ap.dtype)
        pstride = dims[0][0]
        off = ap.offset % pstride if pstride > 0 else ap.offset
        span = 1
        for step, cnt in dims[1:]:
            span += (cnt - 1) * abs(step)
        lo = off * es
        hi = (off + span) * es
        return [(name, b) for b in range(lo // BLK, (hi - 1) // BLK + 1)]

    def add(self, eng, fn, reads=(), writes=(), dma=None, nosame=False):
        idx = len(self.ops)
        deps = {}

        def dep(i):
            o = self.ops[i]
            if nosame and o["dma"] is None and o["eng"] == eng:
                return
            k = ("d", o["dma"]) if o["dma"] is not None else ("e", o["eng"])
            if deps.get(k, -1) < i:
                deps[k] = i

        rk = [k for it in reads for k in self.keys(it)]
        wk = [k for it in writes for k in self.keys(it)]
        for k in rk:
            w = self.lastw.get(k)
            if w is not None:
                dep(w)
        for k in wk:
            w = self.lastw.get(k)
            if w is not None:
                dep(w)
            for r in self.readers.get(k, {}).values():
                dep(r)
        for k in rk:
            self.readers.setdefault(k, {})[("d", dma) if dma is not None else eng] = idx
        for k in wk:
            self.lastw[k] = idx
            self.readers[k] = {}
        self.ops.append(dict(eng=eng, fn=fn, deps=list(deps.values()), dma=dma, inc=False))
        return idx

    def emit(self, block, sems, dma_sems):
        ops = self.ops
        for o in ops:
            for d in o["deps"]:
                p = ops[d]
                if p["dma"] is None and not (p["eng"] == "pe" and o["eng"] == "pe"):
                    p["inc"] = True
        cnt = {e: 0 for e in self.ENG}
        dcnt = {}
        for o in ops:
            if o["dma"] is not None:
                dcnt[o["dma"]] = dcnt.get(o["dma"], 0) + 16
                o["val"] = dcnt[o["dma"]]
            elif o["inc"]:
                cnt[o["eng"]] += 1
                o["val"] = cnt[o["eng"]]
        streams = {e: [] for e in self.ENG}
        for o in ops:
            waits = {}
            for d in o["deps"]:
                p = ops[d]
                if p["dma"] is not None:
                    key = ("d", p["dma"])
                else:
                    if p["eng"] == "pe" and o["eng"] == "pe":
                        continue
                    key = ("e", p["eng"])
                waits[key] = max(waits.get(key, 0), p["val"])
            if o["dma"] is not None and o["val"] > 16:
                key = ("d", o["dma"])
                waits[key] = max(waits.get(key, 0), o["val"] - 16)
            o["waits"] = waits
            streams[o["eng"]].append(o)

        def run(name, eng):
            waited = {}
            for o in streams[name]:
                for key, v in o["waits"].items():
                    if waited.get(key, 0) >= v:
                        continue
                    waited[key] = v
                    s = dma_sems[key[1]] if key[0] == "d" else sems[key[1]]
                    eng.wait_ge(s, v)
                if o["fn"] is None:
                    continue
                inst = o["fn"](eng)
                if o["dma"] is not None:
                    inst.then_inc(dma_sems[o["dma"]], 16)
                elif o["inc"]:
                    inst.then_inc(sems[name], 1)

        block.tensor(lambda e: run("pe", e))
        block.scalar(lambda e: run("act", e))
        block.vector(lambda e: run("dve", e))
        block.gpsimd(lambda e: run("pool", e))
        block.sync(lambda e: run("sp", e))


def _rope_perm(n):
    h = n // 2
    hp = h // 2
    perm = np.zeros(n, np.int64)
    for d in range(n):
        dd = d % h
        perm[d] = d + hp if dd < hp else d - hp
    return perm


def _rope_tables(n):
    h = n // 2
    hp = h // 2
    t = np.arange(L)
    rows = (t // 64).astype(np.float32)
    cols = (t % 64).astype(np.float32)
    freqs = (np.float32(10000.0) ** (-np.arange(hp, dtype=np.float32) / np.float32(hp))).astype(np.float32)
    cos = np.zeros((128, L), np.float32)
    sin = np.zeros((128, L), np.float32)
    for d in range(n):
        part = d // h
        dd = d % h
        j = dd % hp
        pos = rows if part == 0 else cols
        ang = (pos * freqs[j]).astype(np.float32)
        cos[d] = np.cos(ang)
        s = np.sin(ang)
        sin[d] = s if dd >= hp else -s
    if n == 64:
        cos[64:] = cos[:64]
        sin[64:] = sin[:64]
    return np.stack([cos, sin], 0)


def _na_patterns():
    per_t = []
    drs, dcs, vals = [], [], []
    cache = {}
    kq = np.arange(128)
    r_i = kq // 64
    c_i = kq % 64
    for t in range(16):
        if t <= 1:
            cls, kts = t, [0, 1, 2, 3]
        elif t >= 14:
            cls, kts = t, [12, 13, 14, 15]
        else:
            cls, kts = 2, [t - 2, t - 1, t, t + 1, t + 2]
        lst = []
        for kt in kts:
            key = (cls, kt - t)
            if key not in cache:
                qr = 2 * t + r_i[None, :]
                cq = c_i[None, :]
                kr = 2 * kt + r_i[:, None]
                ck = c_i[:, None]
                rs = np.clip(qr - 4, 0, 24)
                vrow = (kr >= rs) & (kr < rs + 8)
                dr = np.clip(kr - qr + 7, 0, 14)
                cs = np.clip(cq - 8, 0, 48)
                vcol = (ck >= cs) & (ck < cs + 16)
                dc = np.clip(ck - cq + 15, 0, 30)
                cache[key] = len(drs)
                drs.append(np.broadcast_to(dr, (128, 128)).copy())
                dcs.append(np.broadcast_to(dc, (128, 128)).copy())
                vals.append((vrow & vcol).astype(np.float32))
            lst.append((kt, cache[key]))
        per_t.append(lst)
    return per_t, np.stack(drs), np.stack(dcs), np.stack(vals)


NA_PER_T, NA_DR, NA_DC, NA_VALID = _na_patterns()
NPAT = NA_DR.shape[0]

VEC_OFF = {}


def _vec_layout():
    off = 0
    for name, n in [("ada_b", 192), ("ln1_g", 32), ("ln1_b", 32), ("ln2_g", 32), ("ln2_b", 32),
                    ("mla_qn", 8), ("mla_kvn", 4), ("na_b", 24), ("gqa_qn", 1), ("gqa_qnP", 1),
                    ("gqa_kn", 1), ("gqa_knP", 1)]:
        VEC_OFF[name] = off
        off += n
    return off


NV = _vec_layout()


def _fm(v):
    v = np.asarray(v, np.float32)
    return np.ascontiguousarray(v.reshape(-1, 128).T)


def build(layers=(0, 1, 2, 3), nb=2, do_mixer=True, do_mlp=True):
    nc = bass.Bass("TRN2", target_bir_lowering=False)

    def din(name, shape):
        return nc.dram_tensor(name, list(shape), F32, kind="ExternalInput").ap()

    x2 = din("x2", [2, L, D])
    ctx2 = din("ctx2", [2, C, D])
    cT = din("cT", [128, 24])
    vecs = din("vecs", [128, NV])
    ident_d = din("ident", [128, 128])
    ada_w = din("ada_w", [4, D, 6 * D])
    mlp_w1 = din("mlp_w1", [4, D, 4 * D])
    mlp_w2 = din("mlp_w2", [4, 4 * D, D])
    mla_w_dq = din("mla_w_dq", [2, D, 512])
    mla_w_uq = din("mla_w_uq", [2, 512, 1536])
    mla_w_uqP = din("mla_w_uqP", [2, 512, 1024])
    mla_w_dkv = din("mla_w_dkv", [2, D, 320])
    mla_w_dkvP = din("mla_w_dkvP", [2, D, 128])
    mla_w_ukv = din("mla_w_ukv", [2, 256, 2048])
    mla_w_o = din("mla_w_o", [2, D, D])
    na_w_qkv = din("na_w_qkv", [1, D, 3072])
    na_w_o = din("na_w_o", [1, D, D])
    na_bv = din("na_bv", [1, 1024])
    na_bias = din("na_bias", [16, 128, NPAT * 128])
    na_mask = din("na_mask", [128, NPAT * 128])
    gqa_w_qkv = din("gqa_w_qkv", [1, D, 1536])
    gqa_w_qkvP = din("gqa_w_qkvP", [1, D, 1280])
    gqa_w_o = din("gqa_w_o", [1, D, D])
    rope_m = din("rope_m", [2, 128, L])
    rope_g = din("rope_g", [2, 128, L])
    y2 = nc.dram_tensor("y2", [2, L, D], F32, kind="ExternalOutput").ap()
    xpark = nc.dram_tensor("xpark", [128, 8, T], F32, kind="Internal").ap()

    S = Sched()
    es = ExitStack()
    with es:
        def sbt(name, shape, dt):
            return es.enter_context(nc.sbuf_tensor(name, shape, dt))

        XR = sbt("XR", [128, 36864], BF16)
        HR = sbt("HR", [128, 18432], BF16)
        NW = 6
        WR = sbt("WR", [128, NW * 4096], BF16)
        TR = sbt("TR", [128, 8192], BF16)
        UR = sbt("UR", [128, 10240], BF16)
        vecs_sb = sbt("vecs_sb", [128, NV], F32)
        modT = sbt("modT", [128, 4 * 48 * 3], F32)
        ident = sbt("ident_sb", [128, 128], F32)
        cT_sb = sbt("cT_sb", [128, 24], F32)
        silu_sb = sbt("silu_sb", [128, 24], F32)
        ones = {}
        for nm in ["1024", "512", "256", "128", "1"]:
            ones[nm] = sbt("ones" + nm, [128, 128], BF16)
        PS = [es.enter_context(nc.psum_tensor(f"ps{i}", [128, 512], F32)) for i in range(8)]
        sems = {e: es.enter_context(nc.semaphore(f"s_{e}")) for e in Sched.ENG}
        NDS = 24
        dma_sems = {i: es.enter_context(nc.semaphore(f"d_{i}")) for i in range(NDS)}
        block = es.enter_context(nc.Block())

        def view(reg, off, shape, dt):
            e = _esize(dt)
            n = int(np.prod(shape))
            a = reg[:, off // 2: off // 2 + n * e // 2]
            if dt != BF16:
                a = a.bitcast(dt)
            if len(shape) == 2:
                a = a.rearrange("p (a b) -> p a b", a=shape[0])
            elif len(shape) == 3:
                a = a.rearrange("p (a b c) -> p a b c", a=shape[0], b=shape[1])
            return a

        xT = view(XR, 0, [8, T], F32)
        hT = view(HR, 0, [8, T], BF16)
        KB = 1024

        def vcol(name, i):
            o = VEC_OFF[name] + i
            return vecs_sb[:, o:o + 1]

        def mod(l, chunk, col):
            o = (l * 48 + chunk) * 3 + col
            return modT[:, o:o + 1]

        def mm(out, lhsT, rhs, start=True, stop=True):
            S.add("pe", lambda e: e.matmul(out, lhsT=lhsT, rhs=rhs, start=start, stop=stop),
                  reads=[lhsT, rhs], writes=[out])

        def tp(out, in_, idn):
            S.add("pe", lambda e: e.transpose(out, in_, idn), reads=[in_, idn], writes=[out])

        def act(out, in_, func, scale=None, bias=None):
            rd = [in_]
            kw = {}
            if scale is not None:
                kw["scale"] = scale
                if not isinstance(scale, float):
                    rd.append(scale)
            if bias is not None:
                kw["bias"] = bias
                if not isinstance(bias, float):
                    rd.append(bias)
            S.add("act", lambda e: e.activation(out=out, in_=in_, func=func, **kw), reads=rd, writes=[out])

        def tt(out, in0, in1, op, eng="dve", nosame=False):
            S.add(eng, lambda e: e.tensor_tensor(out=out, in0=in0, in1=in1, op=op), reads=[in0, in1], writes=[out],
                  nosame=nosame)

        def ts(out, in0, s1, s2, op0, op1=None, eng="dve"):
            rd = [in0] + [s for s in (s1, s2) if s is not None and not isinstance(s, float)]
            if op1 is None:
                S.add(eng, lambda e: e.tensor_scalar(out=out, in0=in0, scalar1=s1, scalar2=None, op0=op0),
                      reads=rd, writes=[out])
            else:
                S.add(eng, lambda e: e.tensor_scalar(out=out, in0=in0, scalar1=s1, scalar2=s2, op0=op0, op1=op1),
                      reads=rd, writes=[out])

        def stt(out, in0, sc, in1, op0, op1):
            rd = [in0, in1] + ([] if isinstance(sc, float) else [sc])
            S.add("dve", lambda e: e.scalar_tensor_tensor(out=out, in0=in0, scalar=sc, in1=in1, op0=op0, op1=op1),
                  reads=rd, writes=[out])

        def recip(out, in_, power=-1.0):
            act(out, in_, AF.Ln)
            act(out, out, AF.Exp, scale=power)

        def copy(out, in_, which):
            if which == 0:
                act(out, in_, AF.Copy)
            else:
                S.add("dve", lambda e: e.tensor_copy(out=out, in_=in_), reads=[in_], writes=[out])

        def dma(q, out, in_, sem, reads=None, writes=None, **kw):
            S.add(q, lambda e: e.dma_start(out=out, in_=in_, **kw),
                  reads=[in_] if reads is None else reads, writes=[out] if writes is None else writes, dma=sem)

        bank_rr = [0]

        def nbank(pool=(0, 1, 2, 3, 4, 5, 6, 7)):
            b = pool[bank_rr[0] % len(pool)]
            bank_rr[0] += 1
            return PS[b]

        wrr = [0]

        def wslot():
            i = wrr[0] % NW
            wrr[0] += 1
            return i

        def wview(slot, shape):
            return view(WR, slot * 8192, shape, BF16)

        def wload(src, shape, slot=None, off=0):
            if slot is None:
                slot = wslot()
            dst = view(WR, slot * 8192 + off, shape, BF16)
            dma("pool", dst, src, slot, max_dma_last_dim=4096)
            return dst

        def rows(w, r0, nchunk, c0, ncol):
            return w[r0:r0 + nchunk * 128, c0:c0 + ncol].rearrange("(c p) n -> p c n", p=128)

        SEM_STG = [6, 7, 8]
        SEM_STG4 = [6, 7, 8, 20]
        SEM_CONST = 9
        SEM_PARK = 10
        SEM_XIN = [11, 12]
        SEM_OUT = [13, 14]
        SEM_TAB = 15
        SEM_E = [16, 17]
        SEM_MISC = 18

        dma("sp", vecs_sb[:], vecs[:, :], SEM_CONST)
        dma("sp", ident[:], ident_d[:, :], SEM_CONST)
        dma("sp", cT_sb[:], cT[:, :], SEM_CONST)
        for nm, val in [("1024", 1.0 / 1024), ("512", 1.0 / 512), ("256", 1.0 / 256), ("128", 1.0 / 128), ("1", 1.0)]:
            t_ = ones[nm]
            S.add("dve", lambda e, t_=t_, val=val: e.memset(t_[:], val), writes=[t_[:]])
        act(silu_sb[:], cT_sb[:], AF.Silu)

        stg = [view(HR, i * 8 * KB, [2048], F32) for i in range(4)]
        modrow = view(XR, 0, [6144], F32)
        stg_i = 0
        for l in range(4):
            if l not in layers:
                continue
            for nbk in range(3):
                banks = [PS[q] for q in range(4)]
                for kc in range(8):
                    r = stg_i % 4
                    stg_i += 1
                    dma("sp", stg[r], ada_w[l, kc * 128:(kc + 1) * 128, nbk * 2048:(nbk + 1) * 2048], SEM_STG4[r])
                    for q in range(4):
                        mm(banks[q][0:3, :], silu_sb[:, kc * 3:(kc + 1) * 3], stg[r][:, q * 512:(q + 1) * 512],
                           start=(kc == 0), stop=(kc == 7))
                for q in range(4):
                    copy(modrow[0:3, (nbk * 4 + q) * 512:(nbk * 4 + q + 1) * 512], banks[q][0:3, :], q % 2)
            pst = PS[4]
            for c in range(48):
                tp(pst[:, c * 3:(c + 1) * 3], modrow[0:3, c * 128:(c + 1) * 128], ident[0:3, 0:3])
            mv = modT[:, l * 144:(l + 1) * 144].rearrange("p (c j) -> p c j", j=3)
            pv = pst[:, 0:144].rearrange("p (c j) -> p c j", j=3)
            ab = vecs_sb[:, VEC_OFF["ada_b"] + l * 48: VEC_OFF["ada_b"] + (l + 1) * 48]
            for j in range(3):
                tt(mv[:, :, j], pv[:, :, j], ab, ALU.add)
            for c0 in (8, 32):
                a = modT[:, (l * 48 + c0) * 3:(l * 48 + c0 + 8) * 3]
                ts(a, a, 1.0, None, ALU.add)
            for c0 in (16, 40):
                a = modT[:, (l * 48 + c0) * 3:(l * 48 + c0 + 8) * 3]
                ts(a, a, 1.0 / ALPHA, None, ALU.mult)

        def modulate(l, bi, sub, src, dst, chunks=TCH):
            sh0 = 0 if sub == 0 else 24
            sc0 = 8 if sub == 0 else 32
            for (c0, w, isctx) in chunks:
                col = 2 if isctx else bi
                for c in range(8):
                    o = dst[:, c, c0:c0 + w]
                    i_ = src[:, c, c0:c0 + w]
                    if c % 2 == 0:
                        act(o, i_, AF.Identity, scale=mod(l, sc0 + c, col), bias=mod(l, sh0 + c, col))
                    else:
                        ts(o, i_, mod(l, sc0 + c, col), mod(l, sh0 + c, col), ALU.mult, ALU.add)

        zb = [view(UR, i * KB, [512], BF16) for i in range(2)]
        zs = [view(UR, (2 + i) * KB, [512], BF16) for i in range(2)]
        mean_sb = view(UR, 4 * KB, [512], F32)
        var_sb = view(UR, 6 * KB, [512], F32)
        sd_sb = view(UR, 8 * KB, [512], F32)
        tbuf = [view(UR, (10 + 2 * i) * KB, [512], F32) for i in range(2)]
        Pr = [view(UR, (14 + i) * KB, [512], BF16) for i in range(4)]
        rs_sb = view(UR, 18 * KB, [512], F32)
        ln_cnt = [0]

        def layernorm(z, w, gname, l):
            pm = nbank()
            pe2 = nbank()
            for m in range(8):
                i = ln_cnt[0] % 2
                ln_cnt[0] += 1
                act(zb[i][:, :w], z[:, m, :w], AF.Copy)
                act(zs[i][:, :w], z[:, m, :w], AF.Square)
                mm(pm[:, :w], ones["1024"][:], zb[i][:, :w], start=(m == 0), stop=(m == 7))
                mm(pe2[:, :w], ones["1024"][:], zs[i][:, :w], start=(m == 0), stop=(m == 7))
            act(mean_sb[:, :w], pm[:, :w], AF.Copy)
            stt(var_sb[:, :w], mean_sb[:, :w], -1.0, mean_sb[:, :w], ALU.mult, ALU.mult)
            stt(var_sb[:, :w], pe2[:, :w], LN_EPS, var_sb[:, :w], ALU.add, ALU.add)
            recip(sd_sb[:, :w], var_sb[:, :w], -0.5)
            for m in range(8):
                t_ = tbuf[m % 2]
                tt(t_[:, :w], z[:, m, :w], mean_sb[:, :w], ALU.subtract)
                tt(t_[:, :w], t_[:, :w], sd_sb[:, :w], ALU.mult)
                act(z[:, m, :w], t_[:, :w], AF.Identity, scale=vcol(gname + "_g", l * 8 + m),
                    bias=vcol(gname + "_b", l * 8 + m))

        def rms_stats(sq_list, w, ones_t):
            pb = nbank()
            n = len(sq_list)
            for i, sq in enumerate(sq_list):
                k = sq.shape[0]
                mm(pb[:, :w], ones_t[0:k, :], sq, start=(i == 0), stop=(i == n - 1))
            ts(var_sb[:, :w], pb[:, :w], RMS_EPS, None, ALU.add)
            recip(sd_sb[:, :w], var_sb[:, :w], -0.5)

        accDs = [view(UR, 4 * KB, [512], F32), view(UR, 0 * KB, [512], F32)]
        accPs = [view(UR, 6 * KB, [512], F32), view(UR, 2 * KB, [512], F32)]
        accb = view(UR, 8 * KB, [512], BF16)

        def attention(qparts, kparts, vfn, kts, qcols, ocb, scale):
            nk = len(kts)
            steps = [(qi, qc, ki, kt) for qi, qc in enumerate(qcols) for ki, kt in enumerate(kts)]
            n = len(steps)
            sb = [PS[0], PS[1], PS[2]]
            ob = [PS[3], PS[4]]
            smb = PS[5]
            deferred = []

            def pv(si):
                qi, (c0, w), ki, kt = steps[si]
                P = Pr[si % 4]
                accD, accP = accDs[qi % 2], accPs[qi % 2]
                mm(ob[qi % 2][:, :w], vfn(kt), P[:, :w], start=(ki == 0), stop=(ki == nk - 1))
                if ki == nk - 1:
                    def fin(w=w, accD=accD, accP=accP, c0=c0, qi=qi):
                        tt(accb[:, :w], accD[:, :w], accP[:, :w], ALU.add)
                        mm(smb[:, :w], ones["1"][:], accb[:, :w])
                        recip(rs_sb[:, :w], smb[:, :w])
                        ocb(c0, w, ob[qi % 2], rs_sb)
                    deferred.append((si + 5, fin))

            for si in range(n):
                qi, (c0, w), ki, kt = steps[si]
                accD, accP = accDs[qi % 2], accPs[qi % 2]
                sp_ = sb[si % 3]
                for i, (qp, kp) in enumerate(zip(qparts, kparts)):
                    mm(sp_[:, :w], kp[:, kt * 128:(kt + 1) * 128], qp[:, c0:c0 + w],
                       start=(i == 0), stop=(i == len(qparts) - 1))
                P = Pr[si % 4]
                act(P[:, :w], sp_[:, :w], AF.Exp, scale=scale)
                acc_ = accD if ki % 2 == 0 else accP
                if ki < 2:
                    S.add("dve", lambda e, P=P, w=w, acc_=acc_: e.tensor_copy(out=acc_[:, :w], in_=P[:, :w]),
                          reads=[P[:, :w]], writes=[acc_[:, :w]])
                else:
                    tt(acc_[:, :w], acc_[:, :w], P[:, :w], ALU.add)
                if si >= 2:
                    pv(si - 2)
                while deferred and deferred[0][0] <= si:
                    deferred.pop(0)[1]()
            for si in range(max(0, n - 2), n):
                pv(si)
            while deferred:
                deferred.pop(0)[1]()

        def phase_c(l, bi, w_o_dram, OT, need_ctx):
            wo = [wload(rows(w_o_dram, 0, 8, h * 512, 512), [8, 512]) for h in range(2)]
            Z = [view(HR, i * 16 * KB, [8, 512], F32) for i in range(2)] + \
                [view(XR, (36 + 16 * i) * KB, [8, 512], F32) for i in range(2)]
            LSEM = [11, 12, 21, 22]
            SSEM = [23, 18]
            chunks = [c for c in TCH if (need_ctx or not c[2])]
            nch = len(chunks)

            def load(ci):
                c0, w, isctx = chunks[ci]
                z = Z[ci % 4]
                dma("sp", z[:, :, :w], xpark[:, :, c0:c0 + w], LSEM[ci % 4],
                    reads=[("xpark", c0)], writes=[z[:, :, :w]])

            for ci in range(min(2, nch)):
                load(ci)
            prev = None
            for ci, (c0, w, isctx) in enumerate(chunks):
                z = Z[ci % 4]
                if ci + 2 < nch:
                    load(ci + 2)
                col = 2 if isctx else bi
                for m in range(8):
                    pb = nbank()
                    for k in range(8):
                        mm(pb[:, :w], wo[m // 4][:, k, (m % 4) * 128:(m % 4 + 1) * 128], OT[:, k, c0:c0 + w],
                           start=(k == 0), stop=(k == 7))
                    stt(z[:, m, :w], pb[:, :w], mod(l, 16 + m, col), z[:, m, :w], ALU.mult, ALU.add)
                if prev is not None:
                    prev()
                def fin(z=z, w=w, c0=c0, ci=ci):
                    layernorm(z, w, "ln1", l)
                    dma("sp", xpark[:, :, c0:c0 + w], z[:, :, :w], SSEM[ci % 2],
                        reads=[z[:, :, :w]], writes=[("xpark", c0)])
                prev = fin
            prev()

        def mlp(l, bi, need_ctx, x_in_dram, hook=None):
            chunks = [c for c in TCH if (need_ctx or not c[2])]
            if x_in_dram:
                for ci, (c0, w, isctx) in enumerate(chunks):
                    dma("sp", xT[:, :, c0:c0 + w], xpark[:, :, c0:c0 + w], SEM_XIN[ci % 2],
                        reads=[("xpark", c0)], writes=[xT[:, :, c0:c0 + w]])
            modulate(l, bi, 1, xT, hT, chunks)
            hid = [view(TR, i * 4 * KB, [4, 512], BF16) for i in range(2)]
            rl = [view(TR, (8 + i) * KB, [512], BF16) for i in range(2)]
            cnt = 0
            ln_q = []
            for j in range(8):
                w1 = wload(rows(mlp_w1[l], 0, 8, j * 512, 512), [8, 512])
                w2 = wload(rows(mlp_w2[l], j * 512, 4, 0, 1024), [4, 1024])
                pend = None
                for ci, (c0, w, isctx) in enumerate(chunks):
                    hd = hid[cnt % 2]
                    cnt += 1
                    for hc in range(4):
                        pb = nbank()
                        for k in range(8):
                            mm(pb[:, :w], w1[:, k, hc * 128:(hc + 1) * 128], hT[:, k, c0:c0 + w],
                               start=(k == 0), stop=(k == 7))
                        r_ = rl[hc % 2]
                        act(r_[:, :w], pb[:, :w], AF.Relu)
                        tt(hd[:, hc, :w], pb[:, :w], r_[:, :w], ALU.mult)
                    if pend is not None:
                        pend()
                    def second(hd=hd, c0=c0, w=w, isctx=isctx, j=j):
                        col = 2 if isctx else bi
                        for m in range(8):
                            pb = nbank()
                            for k in range(4):
                                mm(pb[:, :w], w2[:, k, m * 128:(m + 1) * 128], hd[:, k, :w],
                                   start=(k == 0), stop=(k == 3))
                            stt(xT[:, m, c0:c0 + w], pb[:, :w], mod(l, 40 + m, col), xT[:, m, c0:c0 + w],
                                ALU.mult, ALU.add)
                        if j == 7:
                            ln_q.append((c0, w))
                    pend = second
                    if len(ln_q) >= 2:
                        c0_, w_ = ln_q.pop(0)
                        layernorm(xT[:, :, c0_:c0_ + w_], w_, "ln2", l)
                        if hook is not None:
                            hook(c0_, w_, c0_ == L)
                pend()
            while ln_q:
                c0_, w_ = ln_q.pop(0)
                layernorm(xT[:, :, c0_:c0_ + w_], w_, "ln2", l)
                if hook is not None:
                    hook(c0_, w_, c0_ == L)

        def load_tables(src):
            cos = view(TR, 0, [L], F32)
            sin = view(TR, 8 * KB, [L], F32)
            dma("sp", cos, src[0], SEM_TAB)
            dma("sp", sin, src[1], SEM_TAB)
            return cos, sin

        def park_chunk(c0, w):
            dma("sp", xpark[:, :, c0:c0 + w], xT[:, :, c0:c0 + w], SEM_PARK,
                reads=[xT[:, :, c0:c0 + w]], writes=[("xpark", c0)])

        def park():
            for (c0, w, isctx) in TCH:
                park_chunk(c0, w)

        def phase_a_chunk(l2, bi, c0, w, isctx):
            modulate(l2, bi, 0, xT, hT, [(c0, w, isctx)])
            park_chunk(c0, w)

        def mixer_gqa(l, bi, need_ctx, pre=False):
            j = l // 3
            cos, sin = load_tables(rope_g)
            if not pre:
                modulate(l, bi, 0, xT, hT)
                park()
            OT = view(XR, 0, [8, T], BF16)
            KT = view(XR, 36 * KB, [2, T], BF16)
            VT = view(XR, 45 * KB, [18, 256], BF16)
            QH = [view(XR, (54 + 5 * i) * KB, [T], BF16) for i in range(2)]
            sq = view(UR, 0, [512], BF16)
            ta = view(UR, 10 * KB, [512], F32)
            tb_ = view(UR, 12 * KB, [512], F32)
            scale = 128.0 ** -0.5
            wk = wload(rows(gqa_w_qkv[0], 0, 8, 1024, 256), [8, 256])
            wkP = wload(rows(gqa_w_qkvP[0], 0, 8, 1024, 256), [8, 256])
            wv = wload(rows(gqa_w_qkv[0], 0, 8, 1280, 256), [8, 256])

            def qk_head(w_, wP_, m0, dst, gname, chunks):
                for (c0, w, isctx) in chunks:
                    pq = nbank()
                    for k in range(8):
                        mm(pq[:, :w], w_[:, k, m0:m0 + 128], hT[:, k, c0:c0 + w], start=(k == 0), stop=(k == 7))
                    act(sq[:, :w], pq[:, :w], AF.Square)
                    rms_stats([sq[:, :w]], w, ones["128"])
                    if isctx:
                        stt(dst[:, c0:c0 + w], pq[:, :w], vcol(gname, 0), sd_sb[:, :w], ALU.mult, ALU.mult)
                    else:
                        pp = nbank()
                        for k in range(8):
                            mm(pp[:, :w], wP_[:, k, m0:m0 + 128], hT[:, k, c0:c0 + w], start=(k == 0), stop=(k == 7))
                        stt(ta[:, :w], pq[:, :w], vcol(gname, 0), cos[:, c0:c0 + w], ALU.mult, ALU.mult)
                        stt(tb_[:, :w], pp[:, :w], vcol(gname + "P", 0), sin[:, c0:c0 + w], ALU.mult, ALU.mult)
                        tt(ta[:, :w], ta[:, :w], tb_[:, :w], ALU.add)
                        tt(dst[:, c0:c0 + w], ta[:, :w], sd_sb[:, :w], ALU.mult)

            for kh in range(2):
                qk_head(wk, wkP, kh * 128, KT[:, kh, :], "gqa_kn", TCH)
            for t in range(18):
                if t % 2 == 0:
                    pb = nbank()
                o = pb[:, (t % 2) * 256:(t % 2 + 1) * 256]
                for k in range(8):
                    mm(o, hT[:, k, t * 128:(t + 1) * 128], wv[:, k, :], start=(k == 0), stop=(k == 7))
                if t % 2 == 1:
                    copy(VT[:, t - 1:t + 1, :], pb[:, :].rearrange("p (a b) -> p a b", a=2), (t // 2) % 2)
            qchunks = [c for c in TCH if (need_ctx or not c[2])]
            for h in range(8):
                sl = wslot()
                wq = wload(rows(gqa_w_qkv[0], 0, 8, h * 128, 128), [8, 128], slot=sl, off=0)
                wqP = wload(rows(gqa_w_qkvP[0], 0, 8, h * 128, 128), [8, 128], slot=sl, off=2 * KB)
                qh = QH[h % 2]
                qk_head(wq, wqP, 0, qh, "gqa_qn", qchunks)
                kh = h // 4

                def ocb(c0, w, o_ps, rs, h=h):
                    tt(OT[:, h, c0:c0 + w], o_ps[:, :w], rs[:, :w], ALU.mult)

                attention([qh], [KT[:, kh, :]], lambda kt, kh=kh: VT[:, kt, kh * 128:(kh + 1) * 128],
                          list(range(18)), [(c0, w) for (c0, w, ic) in TCH if not ic], ocb, scale)
                if need_ctx:
                    attention([qh], [KT[:, kh, :]], lambda kt, kh=kh: VT[:, kt, kh * 128:(kh + 1) * 128],
                              [16, 17], [(L, C)], ocb, scale)
            phase_c(l, bi, gqa_w_o[0], OT, need_ctx)

        def mixer_mla(l, bi, need_ctx, pre=False):
            j = l // 3
            cos, sin = load_tables(rope_m)
            if not pre:
                modulate(l, bi, 0, xT, hT)
                park()
            OT = view(XR, 0, [8, T], BF16)
            CQ = view(XR, 36 * KB, [4, T], BF16)
            CKV = view(XR, 54 * KB, [2, T], BF16)
            KPE = view(XR, 63 * KB, [T], BF16)
            sqr = [view(UR, i * KB, [512], BF16) for i in range(4)]
            ta = view(UR, 10 * KB, [512], F32)
            tb_ = view(UR, 12 * KB, [512], F32)
            scale = 192.0 ** -0.5
            wdq = wload(rows(mla_w_dq[j], 0, 8, 0, 512), [8, 512])
            sl = wslot()
            wdkv = wload(rows(mla_w_dkv[j], 0, 8, 0, 320), [8, 320], slot=sl, off=0)
            wdkvP = wload(rows(mla_w_dkvP[j], 0, 8, 0, 128), [8, 128], slot=sl, off=5 * KB)
            S.add("dve", lambda e: e.memset(KPE[0:64, :], 0.0), writes=[KPE[0:64, :]])
            qchunks = [c for c in TCH if (need_ctx or not c[2])]
            for (c0, w, isctx) in TCH:
                if need_ctx or not isctx:
                    pbs = []
                    for m in range(4):
                        pb = nbank()
                        pbs.append(pb)
                        for k in range(8):
                            mm(pb[:, :w], wdq[:, k, m * 128:(m + 1) * 128], hT[:, k, c0:c0 + w],
                               start=(k == 0), stop=(k == 7))
                        act(sqr[m][:, :w], pb[:, :w], AF.Square)
                    rms_stats([sqr[m][:, :w] for m in range(4)], w, ones["512"])
                    for m in range(4):
                        stt(CQ[:, m, c0:c0 + w], pbs[m][:, :w], vcol("mla_qn", j * 4 + m), sd_sb[:, :w],
                            ALU.mult, ALU.mult)
                pbs = []
                for m in range(2):
                    pb = nbank()
                    pbs.append(pb)
                    for k in range(8):
                        mm(pb[:, :w], wdkv[:, k, m * 128:(m + 1) * 128], hT[:, k, c0:c0 + w],
                           start=(k == 0), stop=(k == 7))
                    act(sqr[m][:, :w], pb[:, :w], AF.Square)
                rms_stats([sqr[m][:, :w] for m in range(2)], w, ones["256"])
                for m in range(2):
                    stt(CKV[:, m, c0:c0 + w], pbs[m][:, :w], vcol("mla_kvn", j * 2 + m), sd_sb[:, :w],
                        ALU.mult, ALU.mult)
                pk = nbank()
                for k in range(8):
                    mm(pk[:, :w], wdkv[:, k, 192:320], hT[:, k, c0:c0 + w], start=(k == 0), stop=(k == 7))
                if isctx:
                    copy(KPE[64:128, c0:c0 + w], pk[64:128, :w], 1)
                else:
                    pp = nbank()
                    for k in range(8):
                        mm(pp[:, :w], wdkvP[:, k, :], hT[:, k, c0:c0 + w], start=(k == 0), stop=(k == 7))
                    tt(ta[64:128, :w], pk[64:128, :w], cos[64:128, c0:c0 + w], ALU.mult)
                    tt(tb_[64:128, :w], pp[64:128, :w], sin[64:128, c0:c0 + w], ALU.mult)
                    tt(KPE[64:128, c0:c0 + w], ta[64:128, :w], tb_[64:128, :w], ALU.add)
            wuq = [wload(rows(mla_w_uq[j], 0, 4, g * 768, 768), [4, 768]) for g in range(2)]
            wuqP = wload(rows(mla_w_uqP[j], 0, 4, 0, 1024), [4, 1024])
            sl = wslot()
            for kk in range(2):
                wload(mla_w_ukv[j][kk * 128:(kk + 1) * 128, :], [2048], slot=sl, off=kk * 4 * KB)
            wukv = wview(sl, [2, 2048])
            hb = []
            for i in range(2):
                o = i * 18 * KB
                hb.append(dict(qn=view(HR, o, [T], BF16), qpe=view(HR, o + 4608, [T], BF16),
                               kn=view(HR, o + 9216, [T], BF16), v=view(HR, o + 13824, [18, 128], BF16)))
                qz = hb[-1]["qpe"]
                S.add("dve", lambda e, qz=qz: e.memset(qz[0:64, :], 0.0), writes=[qz[0:64, :]])
            for h in range(8):
                B = hb[h % 2]
                wq = wuq[h // 4]
                hq = (h % 4) * 192
                for (c0, w, isctx) in qchunks:
                    pb = nbank((6, 7))
                    for k in range(4):
                        mm(pb[:, :w], wq[:, k, hq:hq + 128], CQ[:, k, c0:c0 + w], start=(k == 0), stop=(k == 3))
                    copy(B["qn"][:, c0:c0 + w], pb[:, :w], 0)
                    pk = nbank((6, 7))
                    for k in range(4):
                        mm(pk[:, :w], wq[:, k, hq + 64:hq + 192], CQ[:, k, c0:c0 + w], start=(k == 0), stop=(k == 3))
                    if isctx:
                        copy(B["qpe"][64:128, c0:c0 + w], pk[64:128, :w], 1)
                    else:
                        pp = nbank((6, 7))
                        for k in range(4):
                            mm(pp[:, :w], wuqP[:, k, h * 128:(h + 1) * 128], CQ[:, k, c0:c0 + w],
                               start=(k == 0), stop=(k == 3))
                        tt(ta[64:128, :w], pk[64:128, :w], cos[64:128, c0:c0 + w], ALU.mult)
                        tt(tb_[64:128, :w], pp[64:128, :w], sin[64:128, c0:c0 + w], ALU.mult)
                        tt(B["qpe"][64:128, c0:c0 + w], ta[64:128, :w], tb_[64:128, :w], ALU.add)
                for (c0, w, isctx) in TCH:
                    pb = nbank((6, 7))
                    for k in range(2):
                        mm(pb[:, :w], wukv[:, k, h * 256:h * 256 + 128], CKV[:, k, c0:c0 + w],
                           start=(k == 0), stop=(k == 1))
                    copy(B["kn"][:, c0:c0 + w], pb[:, :w], 0)
                for t in range(18):
                    if t % 4 == 0:
                        pb = nbank((6, 7))
                        n4 = min(4, 18 - t)
                    o = pb[:, (t % 4) * 128:(t % 4 + 1) * 128]
                    for k in range(2):
                        mm(o, CKV[:, k, t * 128:(t + 1) * 128], wukv[:, k, h * 256 + 128:h * 256 + 256],
                           start=(k == 0), stop=(k == 1))
                    if t % 4 == n4 - 1:
                        t0 = t - (n4 - 1)
                        copy(B["v"][:, t0:t0 + n4, :], pb[:, :n4 * 128].rearrange("p (a b) -> p a b", a=n4), 1)

                def ocb(c0, w, o_ps, rs, h=h):
                    tt(OT[:, h, c0:c0 + w], o_ps[:, :w], rs[:, :w], ALU.mult)

                attention([B["qn"], B["qpe"]], [B["kn"], KPE], lambda kt, B=B: B["v"][:, kt, :],
                          list(range(18)), [(c0, w) for (c0, w, ic) in TCH if not ic], ocb, scale)
                if need_ctx:
                    attention([B["qn"], B["qpe"]], [B["kn"], KPE], lambda kt, B=B: B["v"][:, kt, :],
                              [16, 17], [(L, C)], ocb, scale)
            phase_c(l, bi, mla_w_o[j], OT, need_ctx)

        def mixer_na(l, bi, need_ctx, pre=False):
            if not pre:
                modulate(l, bi, 0, xT, hT)
                park()
            OT = view(XR, 0, [8, T], BF16)
            cb = []
            for i in range(2):
                o = (36 + 18 * i) * KB
                cb.append(dict(qA=view(XR, o, [T], BF16), qB=view(XR, o + 4608, [T], BF16),
                               k=view(XR, o + 9216, [T], BF16), v=view(XR, o + 13824, [18, 128], BF16)))
                qa, qb_ = cb[-1]["qA"], cb[-1]["qB"]
                S.add("dve", lambda e, qa=qa: e.memset(qa[64:128, :], 0.0), writes=[qa[64:128, :]])
                S.add("dve", lambda e, qb_=qb_: e.memset(qb_[0:64, :], 0.0), writes=[qb_[0:64, :]])
            otmp = view(UR, 10 * KB, [512], F32)
            maskt = view(TR, 0, [NPAT * 128], BF16)
            dma("pool", maskt, na_mask[:, :], 19, max_dma_last_dim=4096)
            Et = [view(TR, (5376 * (1 + i)), [NPAT * 128], BF16) for i in range(2)]
            Pn = [view(UR, (14 + 2 * i) * KB, [896], BF16) for i in range(2)]
            scale = 64.0 ** -0.5
            pbase = {}
            for t in range(16):
                pbase[t] = NA_PER_T[t][0][1]
            for m in range(8):
                B = cb[m % 2]
                sl = wslot()
                wq = wload(rows(na_w_qkv[0], 0, 8, m * 128, 128), [8, 128], slot=sl, off=0)
                wk = wload(rows(na_w_qkv[0], 0, 8, 1024 + m * 128, 128), [8, 128], slot=sl, off=2 * KB)
                wv = wload(rows(na_w_qkv[0], 0, 8, 2048 + m * 128, 128), [8, 128], slot=sl, off=4 * KB)
                for (c0, w, isctx) in TCH:
                    pb = nbank((6, 7))
                    for k in range(8):
                        mm(pb[:, :w], wq[:, k, :], hT[:, k, c0:c0 + w], start=(k == 0), stop=(k == 7))
                    act(B["qA"][0:64, c0:c0 + w], pb[0:64, :w], AF.Identity, bias=vcol("na_b", m)[0:64, :])
                    act(B["qB"][64:128, c0:c0 + w], pb[64:128, :w], AF.Identity, bias=vcol("na_b", m)[64:128, :])
                    pb = nbank((6, 7))
                    for k in range(8):
                        mm(pb[:, :w], wk[:, k, :], hT[:, k, c0:c0 + w], start=(k == 0), stop=(k == 7))
                    ts(B["k"][:, c0:c0 + w], pb[:, :w], vcol("na_b", 8 + m), None, ALU.add)
                for t in range(18):
                    if t % 4 == 0:
                        pb = nbank((6, 7))
                        n4 = min(4, 18 - t)
                    o = pb[:, (t % 4) * 128:(t % 4 + 1) * 128]
                    for k in range(8):
                        mm(o, hT[:, k, t * 128:(t + 1) * 128], wv[:, k, :], start=(k == 0), stop=(k == 7))
                    if t % 4 == n4 - 1:
                        t0 = t - (n4 - 1)
                        copy(B["v"][:, t0:t0 + n4, :], pb[:, :n4 * 128].rearrange("p (a b) -> p a b", a=n4), 1)
                for e_ in range(2):
                    hd = 2 * m + e_
                    E = Et[hd % 2]
                    dma("pool", E, na_bias[hd], SEM_E[hd % 2], max_dma_last_dim=4096)
                    act(E, E, AF.Exp)
                    tt(E, E, maskt, ALU.mult)
                    p0, p1 = e_ * 64, (e_ + 1) * 64
                    Bq = B["qA"] if e_ == 0 else B["qB"]
                    def na_pv(t, tiles, P):
                        nt = len(tiles)
                        g = t // 4
                        oc = (t % 4) * 128
                        for i, kt in enumerate(tiles):
                            mm(ob[g % 2][:, oc:oc + 128], B["v"][:, kt, :], P[:, i * 128:(i + 1) * 128],
                               start=(i == 0), stop=(i == nt - 1))
                        for i, kt in enumerate(tiles):
                            mm(smb[g % 2][:, oc:oc + 128], ones["1"][:], P[:, i * 128:(i + 1) * 128],
                               start=(i == 0), stop=(i == nt - 1))
                        if t % 4 == 3:
                            recip(rs_sb[p0:p1, :], smb[g % 2][p0:p1, :])
                            tt(otmp[p0:p1, :], ob[g % 2][p0:p1, :], rs_sb[p0:p1, :], ALU.mult)
                            ts(OT[p0:p1, m, g * 512:(g + 1) * 512], otmp[p0:p1, :], vcol("na_b", 16 + m)[p0:p1, :], None,
                               ALU.add)

                    ob = [PS[4], PS[5]]
                    smb = [PS[6], PS[7]]
                    pend_na = None
                    for t in range(16):
                        lst = NA_PER_T[t]
                        nl = len(lst)
                        tiles = [kt for kt, _ in lst] + [16, 17]
                        nt = len(tiles)
                        P = Pn[t % 2]
                        sbk = [PS[0], PS[1]] if t % 2 == 0 else [PS[2], PS[3]]
                        for i, kt in enumerate(tiles):
                            mm(sbk[i // 4][:, (i % 4) * 128:(i % 4 + 1) * 128], B["k"][:, kt * 128:(kt + 1) * 128],
                               Bq[:, t * 128:(t + 1) * 128])
                        act(P[:, 0:512], sbk[0][:, :], AF.Exp, scale=scale)
                        act(P[:, 512:nt * 128], sbk[1][:, 0:(nt - 4) * 128], AF.Exp, scale=scale)
                        eb = pbase[t] * 128
                        tt(P[:, 0:nl * 128], P[:, 0:nl * 128], E[:, eb:eb + nl * 128], ALU.mult)
                        if pend_na is not None:
                            pend_na()
                        pend_na = (lambda t=t, tiles=tiles, P=P: na_pv(t, tiles, P))
                    if pend_na is not None:
                        pend_na()
                    if need_ctx:
                        P = Pn[0]
                        for i, kt in enumerate((16, 17)):
                            mm(PS[0][:, i * 256:(i + 1) * 256], B["k"][:, kt * 128:(kt + 1) * 128],
                               Bq[:, L:L + C])
                        act(P[:, 0:512], PS[0][:, :], AF.Exp, scale=scale)
                        for i, kt in enumerate((16, 17)):
                            mm(ob[0][:, 0:256], B["v"][:, kt, :], P[:, i * 256:(i + 1) * 256],
                               start=(i == 0), stop=(i == 1))
                        for i, kt in enumerate((16, 17)):
                            mm(smb[0][:, 0:256], ones["1"][:], P[:, i * 256:(i + 1) * 256],
                               start=(i == 0), stop=(i == 1))
                        recip(rs_sb[p0:p1, 0:256], smb[0][p0:p1, 0:256])
                        tt(otmp[p0:p1, 0:256], ob[0][p0:p1, 0:256], rs_sb[p0:p1, 0:256], ALU.mult)
                        ts(OT[p0:p1, m, L:L + C], otmp[p0:p1, 0:256], vcol("na_b", 16 + m)[p0:p1, :], None, ALU.add)
            phase_c(l, bi, na_w_o[0], OT, need_ctx)

        xs = [view(HR, i * 4 * KB, [1024], F32) for i in range(3)]
        ost = [view(HR, (16 + i * 4) * KB, [1024], F32) for i in range(2)]
        for bi in range(nb):
            for t in range(18):
                r = t % 3
                src = x2[bi, t * 128:(t + 1) * 128, :] if t < 16 else ctx2[bi, (t - 16) * 128:(t - 15) * 128, :]
                dma("sp", xs[r], src, SEM_STG[r])
                for half in range(2):
                    pb = nbank()
                    for c in range(4):
                        cc = half * 4 + c
                        tp(pb[:, c * 128:(c + 1) * 128], xs[r][:, cc * 128:(cc + 1) * 128], ident[:])
                    copy(xT[:, half * 4:(half + 1) * 4, t * 128:(t + 1) * 128],
                         pb[:, :].rearrange("p (a b) -> p a b", a=4), half)
            for li, l in enumerate(layers):
                kind = l % 3
                need_ctx = l < DEPTH - 1
                pre = li > 0 and do_mixer and do_mlp
                if do_mixer:
                    if kind == 0:
                        mixer_mla(l, bi, need_ctx, pre)
                    elif kind == 1:
                        mixer_na(l, bi, need_ctx, pre)
                    else:
                        mixer_gqa(l, bi, need_ctx, pre)
                if do_mlp:
                    hook = None
                    if do_mixer and li + 1 < len(layers):
                        l2 = layers[li + 1]
                        hook = (lambda c0, w, ic, l2=l2, bi=bi: phase_a_chunk(l2, bi, c0, w, ic))
                    mlp(l, bi, need_ctx, x_in_dram=do_mixer, hook=hook)
                elif do_mixer:
                    for ci, (c0, w, isctx) in enumerate(TCH):
                        if need_ctx or not isctx:
                            dma("sp", xT[:, :, c0:c0 + w], xpark[:, :, c0:c0 + w], SEM_XIN[ci % 2],
                                reads=[("xpark", c0)], writes=[xT[:, :, c0:c0 + w]])
            for t in range(16):
                o_ = ost[t % 2]
                for half in range(2):
                    pb = nbank()
                    for c in range(4):
                        cc = half * 4 + c
                        tp(pb[:, c * 128:(c + 1) * 128], xT[:, cc, t * 128:(t + 1) * 128], ident[:])
                    copy(o_[:, half * 512:(half + 1) * 512], pb[:, :], half)
                dma("sp", y2[bi, t * 128:(t + 1) * 128, :], o_, SEM_OUT[t % 2], reads=[o_], writes=[("y2", bi, t)])
        S.add("sp", None, reads=[("y2", b_, t) for b_ in range(nb) for t in range(16)])
        S.emit(block, sems, dma_sems)
    return nc


def make_in_maps(inp, ncores=8):
    f = lambda k: np.ascontiguousarray(np.asarray(inp[k], np.float32))
    p64 = _rope_perm(64)
    p128 = _rope_perm(128)
    w_uq = f("mla_w_uq")
    pe_cols = w_uq.reshape(2, 512, 8, 192)[:, :, :, 128:]
    w_uqP = np.ascontiguousarray(np.concatenate([pe_cols, pe_cols[:, :, :, p64]], axis=3).reshape(2, 512, 1024))
    w_dkv = f("mla_w_dkv")
    w_dkvP = np.ascontiguousarray(np.concatenate([w_dkv[:, :, 256:], w_dkv[:, :, 256:][:, :, p64]], axis=2))
    gw = f("gqa_w_qkv")
    gwP = np.ascontiguousarray(gw[:, :, :1280].reshape(1, D, 10, 128)[:, :, :, p128].reshape(1, D, 1280))
    rpb = f("na_rpb")[0]
    nbias = rpb[:, NA_DR, NA_DC]
    nbias = np.ascontiguousarray(nbias.transpose(0, 2, 1, 3).reshape(16, 128, NPAT * 128))
    nmask = np.ascontiguousarray(NA_VALID.transpose(1, 0, 2).reshape(128, NPAT * 128))
    bq = f("na_b_qkv")
    qn = f("gqa_q_norm")[0]
    kn = f("gqa_k_norm")[0]
    vec = np.concatenate([
        _fm(f("ada_b")), _fm(f("ln1_g")), _fm(f("ln1_b")), _fm(f("ln2_g")), _fm(f("ln2_b")),
        _fm(f("mla_q_norm")), _fm(f("mla_kv_norm")), _fm(bq[0]),
        _fm(qn), _fm(qn[p128]), _fm(kn), _fm(kn[p128])], axis=1)
    assert vec.shape == (128, NV), vec.shape
    shared = dict(
        vecs=np.ascontiguousarray(vec), ident=np.eye(128, dtype=np.float32),
        ada_w=f("ada_w"), mlp_w1=f("mlp_w1"), mlp_w2=f("mlp_w2"),
        mla_w_dq=f("mla_w_dq"), mla_w_uq=w_uq, mla_w_uqP=w_uqP, mla_w_dkv=w_dkv, mla_w_dkvP=w_dkvP,
        mla_w_ukv=f("mla_w_ukv"), mla_w_o=f("mla_w_o"),
        na_w_qkv=f("na_w_qkv"), na_w_o=f("na_w_o"), na_bv=np.ascontiguousarray(bq[:, 2048:]),
        na_bias=nbias, na_mask=nmask,
        gqa_w_qkv=gw, gqa_w_qkvP=gwP, gqa_w_o=f("gqa_w_o"),
        rope_m=_rope_tables(64), rope_g=_rope_tables(128),
    )
    x = f("x")
    ctx = f("ctx")
    c = f("c")
    cc = f("c_ctx")
    maps = []
    for i in range(ncores):
        c3 = np.stack([c[2 * i], c[2 * i + 1], cc], 0)
        cTm = np.ascontiguousarray(c3.reshape(3, 8, 128).transpose(2, 1, 0).reshape(128, 24))
        m = dict(shared)
        m["x2"] = np.ascontiguousarray(x[2 * i:2 * i + 2])
        m["ctx2"] = np.ascontiguousarray(ctx[2 * i:2 * i + 2])
        m["cT"] = cTm
        maps.append(m)
    return maps


_NC = None


def kernel(**inputs):
    global _NC
    if _NC is None:
        _NC = build()
    maps = make_in_maps(inputs, 8)
    res = run_bass_kernel_spmd(_NC, maps, core_ids=list(range(8)))
    out = np.concatenate([np.asarray(r["y2"], np.float32) for r in res.results], axis=0)
    return out
```

```python
import math
from contextlib import ExitStack
import numpy as np
import concourse.bass as bass
import concourse.mybir as mybir
from concourse.bass_utils import run_bass_kernel_spmd

F32 = mybir.dt.float32
BF16 = mybir.dt.bfloat16
AF = mybir.ActivationFunctionType
ALU = mybir.AluOpType

D = 1024
L = 2048
C = 256
T = L + C
DEPTH = 4
ALPHA = (2.0 * DEPTH) ** 0.25
LN_EPS = 1e-5 / (ALPHA * ALPHA)
RMS_EPS = 1e-6
TCH = [(0, 512, False), (512, 512, False), (1024, 512, False), (1536, 512, False), (2048, 256, True)]
BLK = 256


def _esize(dt):
    return 2 if dt == BF16 else 4


class Sched:
    ENG = ["pe", "act", "dve", "pool", "sp"]

    def __init__(self):
        self.ops = []
        self.lastw = {}
        self.readers = {}

    @staticmethod
    def keys(item):
        if isinstance(item, (str, tuple)):
            return [item]
        ap = item
        name = ap.tensor.name
        if str(ap.space) == "DRAM":
            return [name]
        dims = ap.ap
        es = _esize(ap.dtype)
        pstride = dims[0][0]
        off = ap.offset % pstride if pstride > 0 else ap.offset
        span = 1
        for step, cnt in dims[1:]:
            span += (cnt - 1) * abs(step)
        lo = off * es
        hi = (off + span) * es
        return [(name, b) for b in range(lo // BLK, (hi - 1) // BLK + 1)]

    def add(self, eng, fn, reads=(), writes=(), dma=None, nosame=False):
        idx = len(self.ops)
        deps = {}

        def dep(i):
            o = self.ops[i]
            if nosame and o["dma"] is None and o["eng"] == eng:
                return
            k = ("d", o["dma"]) if o["dma"] is not None else ("e", o["eng"])
            if deps.get(k, -1) < i:
                deps[k] = i

        rk = [k for it in reads for k in self.keys(it)]
        wk = [k for it in writes for k in self.keys(it)]
        for k in rk:
            w = self.lastw.get(k)
            if w is not None:
                dep(w)
        for k in wk:
            w = self.lastw.get(k)
            if w is not None:
                dep(w)
            for r in self.readers.get(k, {}).values():
                dep(r)
        for k in rk:
            self.readers.setdefault(k, {})[("d", dma) if dma is not None else eng] = idx
        for k in wk:
            self.lastw[k] = idx
            self.readers[k] = {}
        self.ops.append(dict(eng=eng, fn=fn, deps=list(deps.values()), dma=dma, inc=False))
        return idx

    def emit(self, block, sems, dma_sems):
        ops = self.ops
        for o in ops:
            for d in o["deps"]:
                p = ops[d]
                if p["dma"] is None and not (p["eng"] == "pe" and o["eng"] == "pe"):
                    p["inc"] = True
        cnt = {e: 0 for e in self.ENG}
        dcnt = {}
        for o in ops:
            if o["dma"] is not None:
                dcnt[o["dma"]] = dcnt.get(o["dma"], 0) + 16
                o["val"] = dcnt[o["dma"]]
            elif o["inc"]:
                cnt[o["eng"]] += 1
                o["val"] = cnt[o["eng"]]
        streams = {e: [] for e in self.ENG}
        for o in ops:
            waits = {}
            for d in o["deps"]:
                p = ops[d]
                if p["dma"] is not None:
                    key = ("d", p["dma"])
                else:
                    if p["eng"] == "pe" and o["eng"] == "pe":
                        continue
                    key = ("e", p["eng"])
                waits[key] = max(waits.get(key, 0), p["val"])
            if o["dma"] is not None and o["val"] > 16:
                key = ("d", o["dma"])
                waits[key] = max(waits.get(key, 0), o["val"] - 16)
            o["waits"] = waits
            streams[o["eng"]].append(o)

        def run(name, eng):
            waited = {}
            for o in streams[name]:
                for key, v in o["waits"].items():
                    if waited.get(key, 0) >= v:
                        continue
                    waited[key] = v
                    s = dma_sems[key[1]] if key[0] == "d" else sems[key[1]]
                    eng.wait_ge(s, v)
                if o["fn"] is None:
                    continue
                inst = o["fn"](eng)
                if o["dma"] is not None:
                    inst.then_inc(dma_sems[o["dma"]], 16)
                elif o["inc"]:
                    inst.then_inc(sems[name], 1)

        block.tensor(lambda e: run("pe", e))
        block.scalar(lambda e: run("act", e))
        block.vector(lambda e: run("dve", e))
        block.gpsimd(lambda e: run("pool", e))
        block.sync(lambda e: run("sp", e))


def _rope_perm(n):
    h = n // 2
    hp = h // 2
    perm = np.zeros(n, np.int64)
    for d in range(n):
        dd = d % h
        perm[d] = d + hp if dd < hp else d - hp
    return perm


def _rope_tables(n):
    h = n // 2
    hp = h // 2
    t = np.arange(L)
    rows = (t // 64).astype(np.float32)
    cols = (t % 64).astype(np.float32)
    freqs = (np.float32(10000.0) ** (-np.arange(hp, dtype=np.float32) / np.float32(hp))).astype(np.float32)
    cos = np.zeros((128, L), np.float32)
    sin = np.zeros((128, L), np.float32)
    for d in range(n):
        part = d // h
        dd = d % h
        j = dd % hp
        pos = rows if part == 0 else cols
        ang = (pos * freqs[j]).astype(np.float32)
        cos[d] = np.cos(ang)
        s = np.sin(ang)
        sin[d] = s if dd >= hp else -s
    if n == 64:
        cos[64:] = cos[:64]
        sin[64:] = sin[:64]
    return np.stack([cos, sin], 0)


def _na_patterns():
    per_t = []
    drs, dcs, vals = [], [], []
    cache = {}
    kq = np.arange(128)
    r_i = kq // 64
    c_i = kq % 64
    for t in range(16):
        if t <= 1:
            cls, kts = t, [0, 1, 2, 3]
        elif t >= 14:
            cls, kts = t, [12, 13, 14, 15]
        else:
            cls, kts = 2, [t - 2, t - 1, t, t + 1, t + 2]
        lst = []
        for kt in kts:
            key = (cls, kt - t)
            if key not in cache:
                qr = 2 * t + r_i[None, :]
                cq = c_i[None, :]
                kr = 2 * kt + r_i[:, None]
                ck = c_i[:, None]
                rs = np.clip(qr - 4, 0, 24)
                vrow = (kr >= rs) & (kr < rs + 8)
                dr = np.clip(kr - qr + 7, 0, 14)
                cs = np.clip(cq - 8, 0, 48)
                vcol = (ck >= cs) & (ck < cs + 16)
                dc = np.clip(ck - cq + 15, 0, 30)
                cache[key] = len(drs)
                drs.append(np.broadcast_to(dr, (128, 128)).copy())
                dcs.append(np.broadcast_to(dc, (128, 128)).copy())
                vals.append((vrow & vcol).astype(np.float32))
            lst.append((kt, cache[key]))
        per_t.append(lst)
    return per_t, np.stack(drs), np.stack(dcs), np.stack(vals)


NA_PER_T, NA_DR, NA_DC, NA_VALID = _na_patterns()
NPAT = NA_DR.shape[0]

VEC_OFF = {}


def _vec_layout():
    off = 0
    for name, n in [("ada_b", 192), ("ln1_g", 32), ("ln1_b", 32), ("ln2_g", 32), ("ln2_b", 32),
                    ("mla_qn", 8), ("mla_kvn", 4), ("na_b", 24), ("gqa_qn", 1), ("gqa_qnP", 1),
                    ("gqa_kn", 1), ("gqa_knP", 1)]:
        VEC_OFF[name] = off
        off += n
    return off


NV = _vec_layout()


def _fm(v):
    v = np.asarray(v, np.float32)
    return np.ascontiguousarray(v.reshape(-1, 128).T)


def build(layers=(0, 1, 2, 3), nb=2, do_mixer=True, do_mlp=True):
    nc = bass.Bass("TRN2", target_bir_lowering=False)

    def din(name, shape):
        return nc.dram_tensor(name, list(shape), F32, kind="ExternalInput").ap()

    x2 = din("x2", [2, L, D])
    ctx2 = din("ctx2", [2, C, D])
    cT = din("cT", [128, 24])
    vecs = din("vecs", [128, NV])
    ident_d = din("ident", [128, 128])
    ada_w = din("ada_w", [4, D, 6 * D])
    mlp_w1 = din("mlp_w1", [4, D, 4 * D])
    mlp_w2 = din("mlp_w2", [4, 4 * D, D])
    mla_w_dq = din("mla_w_dq", [2, D, 512])
    mla_w_uq = din("mla_w_uq", [2, 512, 1536])
    mla_w_uqP = din("mla_w_uqP", [2, 512, 1024])
    mla_w_dkv = din("mla_w_dkv", [2, D, 320])
    mla_w_dkvP = din("mla_w_dkvP", [2, D, 128])
    mla_w_ukv = din("mla_w_ukv", [2, 256, 2048])
    mla_w_o = din("mla_w_o", [2, D, D])
    na_w_qkv = din("na_w_qkv", [1, D, 3072])
    na_w_o = din("na_w_o", [1, D, D])
    na_bv = din("na_bv", [1, 1024])
    na_bias = din("na_bias", [16, 128, NPAT * 128])
    na_mask = din("na_mask", [128, NPAT * 128])
    gqa_w_qkv = din("gqa_w_qkv", [1, D, 1536])
    gqa_w_qkvP = din("gqa_w_qkvP", [1, D, 1280])
    gqa_w_o = din("gqa_w_o", [1, D, D])
    rope_m = din("rope_m", [2, 128, L])
    rope_g = din("rope_g", [2, 128, L])
    y2 = nc.dram_tensor("y2", [2, L, D], F32, kind="ExternalOutput").ap()
    xpark = nc.dram_tensor("xpark", [128, 8, T], F32, kind="Internal").ap()

    S = Sched()
    es = ExitStack()
    with es:
        def sbt(name, shape, dt):
            return es.enter_context(nc.sbuf_tensor(name, shape, dt))

        XR = sbt("XR", [128, 36864], BF16)
        HR = sbt("HR", [128, 18432], BF16)
        NW = 6
        WR = sbt("WR", [128, NW * 4096], BF16)
        TR = sbt("TR", [128, 8192], BF16)
        UR = sbt("UR", [128, 10240], BF16)
        vecs_sb = sbt("vecs_sb", [128, NV], F32)
        modT = sbt("modT", [128, 4 * 48 * 3], F32)
        ident = sbt("ident_sb", [128, 128], F32)
        cT_sb = sbt("cT_sb", [128, 24], F32)
        silu_sb = sbt("silu_sb", [128, 24], F32)
        ones = {}
        for nm in ["1024", "512", "256", "128", "1"]:
            ones[nm] = sbt("ones" + nm, [128, 128], BF16)
        PS = [es.enter_context(nc.psum_tensor(f"ps{i}", [128, 512], F32)) for i in range(8)]
        sems = {e: es.enter_context(nc.semaphore(f"s_{e}")) for e in Sched.ENG}
        NDS = 24
        dma_sems = {i: es.enter_context(nc.semaphore(f"d_{i}")) for i in range(NDS)}
        block = es.enter_context(nc.Block())

        def view(reg, off, shape, dt):
            e = _esize(dt)
            n = int(np.prod(shape))
            a = reg[:, off // 2: off // 2 + n * e // 2]
            if dt != BF16:
                a = a.bitcast(dt)
            if len(shape) == 2:
                a = a.rearrange("p (a b) -> p a b", a=shape[0])
            elif len(shape) == 3:
                a = a.rearrange("p (a b c) -> p a b c", a=shape[0], b=shape[1])
            return a

        xT = view(XR, 0, [8, T], F32)
        hT = view(HR, 0, [8, T], BF16)
        KB = 1024

        def vcol(name, i):
            o = VEC_OFF[name] + i
            return vecs_sb[:, o:o + 1]

        def mod(l, chunk, col):
            o = (l * 48 + chunk) * 3 + col
            return modT[:, o:o + 1]

        def mm(out, lhsT, rhs, start=True, stop=True):
            S.add("pe", lambda e: e.matmul(out, lhsT=lhsT, rhs=rhs, start=start, stop=stop),
                  reads=[lhsT, rhs], writes=[out])

        def tp(out, in_, idn):
            S.add("pe", lambda e: e.transpose(out, in_, idn), reads=[in_, idn], writes=[out])

        def act(out, in_, func, scale=None, bias=None):
            rd = [in_]
            kw = {}
            if scale is not None:
                kw["scale"] = scale
                if not isinstance(scale, float):
                    rd.append(scale)
            if bias is not None:
                kw["bias"] = bias
                if not isinstance(bias, float):
                    rd.append(bias)
            S.add("act", lambda e: e.activation(out=out, in_=in_, func=func, **kw), reads=rd, writes=[out])

        def tt(out, in0, in1, op, eng="dve", nosame=False):
            S.add(eng, lambda e: e.tensor_tensor(out=out, in0=in0, in1=in1, op=op), reads=[in0, in1], writes=[out],
                  nosame=nosame)

        def ts(out, in0, s1, s2, op0, op1=None, eng="dve"):
            rd = [in0] + [s for s in (s1, s2) if s is not None and not isinstance(s, float)]
            if op1 is None:
                S.add(eng, lambda e: e.tensor_scalar(out=out, in0=in0, scalar1=s1, scalar2=None, op0=op0),
                      reads=rd, writes=[out])
            else:
                S.add(eng, lambda e: e.tensor_scalar(out=out, in0=in0, scalar1=s1, scalar2=s2, op0=op0, op1=op1),
                      reads=rd, writes=[out])

        def stt(out, in0, sc, in1, op0, op1):
            rd = [in0, in1] + ([] if isinstance(sc, float) else [sc])
            S.add("dve", lambda e: e.scalar_tensor_tensor(out=out, in0=in0, scalar=sc, in1=in1, op0=op0, op1=op1),
                  reads=rd, writes=[out])

        def recip(out, in_, power=-1.0):
            act(out, in_, AF.Ln)
            act(out, out, AF.Exp, scale=power)

        def copy(out, in_, which):
            if which == 0:
                act(out, in_, AF.Copy)
            else:
                S.add("dve", lambda e: e.tensor_copy(out=out, in_=in_), reads=[in_], writes=[out])

        def dma(q, out, in_, sem, reads=None, writes=None, **kw):
            S.add(q, lambda e: e.dma_start(out=out, in_=in_, **kw),
                  reads=[in_] if reads is None else reads, writes=[out] if writes is None else writes, dma=sem)

        bank_rr = [0]

        def nbank(pool=(0, 1, 2, 3, 4, 5, 6, 7)):
            b = pool[bank_rr[0] % len(pool)]
            bank_rr[0] += 1
            return PS[b]

        wrr = [0]

        def wslot():
            i = wrr[0] % NW
            wrr[0] += 1
            return i

        def wview(slot, shape):
            return view(WR, slot * 8192, shape, BF16)

        def wload(src, shape, slot=None, off=0):
            if slot is None:
                slot = wslot()
            dst = view(WR, slot * 8192 + off, shape, BF16)
            dma("pool", dst, src, slot, max_dma_last_dim=4096)
            return dst

        def rows(w, r0, nchunk, c0, ncol):
            return w[r0:r0 + nchunk * 128, c0:c0 + ncol].rearrange("(c p) n -> p c n", p=128)

        SEM_STG = [6, 7, 8]
        SEM_STG4 = [6, 7, 8, 20]
        SEM_CONST = 9
        SEM_PARK = 10
        SEM_XIN = [11, 12]
        SEM_OUT = [13, 14]
        SEM_TAB = 15
        SEM_E = [16, 17]
        SEM_MISC = 18

        dma("sp", vecs_sb[:], vecs[:, :], SEM_CONST)
        dma("sp", ident[:], ident_d[:, :], SEM_CONST)
        dma("sp", cT_sb[:], cT[:, :], SEM_CONST)
        for nm, val in [("1024", 1.0 / 1024), ("512", 1.0 / 512), ("256", 1.0 / 256), ("128", 1.0 / 128), ("1", 1.0)]:
            t_ = ones[nm]
            S.add("dve", lambda e, t_=t_, val=val: e.memset(t_[:], val), writes=[t_[:]])
        act(silu_sb[:], cT_sb[:], AF.Silu)

        stg = [view(HR, i * 8 * KB, [2048], F32) for i in range(4)]
        modrow = view(XR, 0, [6144], F32)
        stg_i = 0
        for l in range(4):
            if l not in layers:
                continue
            for nbk in range(3):
                banks = [PS[q] for q in range(4)]
                for kc in range(8):
                    r = stg_i % 4
                    stg_i += 1
                    dma("sp", stg[r], ada_w[l, kc * 128:(kc + 1) * 128, nbk * 2048:(nbk + 1) * 2048], SEM_STG4[r])
                    for q in range(4):
                        mm(banks[q][0:3, :], silu_sb[:, kc * 3:(kc + 1) * 3], stg[r][:, q * 512:(q + 1) * 512],
                           start=(kc == 0), stop=(kc == 7))
                for q in range(4):
                    copy(modrow[0:3, (nbk * 4 + q) * 512:(nbk * 4 + q + 1) * 512], banks[q][0:3, :], q % 2)
            pst = PS[4]
            for c in range(48):
                tp(pst[:, c * 3:(c + 1) * 3], modrow[0:3, c * 128:(c + 1) * 128], ident[0:3, 0:3])
            mv = modT[:, l * 144:(l + 1) * 144].rearrange("p (c j) -> p c j", j=3)
            pv = pst[:, 0:144].rearrange("p (c j) -> p c j", j=3)
            ab = vecs_sb[:, VEC_OFF["ada_b"] + l * 48: VEC_OFF["ada_b"] + (l + 1) * 48]
            for j in range(3):
                tt(mv[:, :, j], pv[:, :, j], ab, ALU.add)
            for c0 in (8, 32):
                a = modT[:, (l * 48 + c0) * 3:(l * 48 + c0 + 8) * 3]
                ts(a, a, 1.0, None, ALU.add)
            for c0 in (16, 40):
                a = modT[:, (l * 48 + c0) * 3:(l * 48 + c0 + 8) * 3]
                ts(a, a, 1.0 / ALPHA, None, ALU.mult)

        def modulate(l, bi, sub, src, dst, chunks=TCH):
            sh0 = 0 if sub == 0 else 24
            sc0 = 8 if sub == 0 else 32
            for (c0, w, isctx) in chunks:
                col = 2 if isctx else bi
                for c in range(8):
                    o = dst[:, c, c0:c0 + w]
                    i_ = src[:, c, c0:c0 + w]
                    if c % 2 == 0:
                        act(o, i_, AF.Identity, scale=mod(l, sc0 + c, col), bias=mod(l, sh0 + c, col))
                    else:
                        ts(o, i_, mod(l, sc0 + c, col), mod(l, sh0 + c, col), ALU.mult, ALU.add)

        zb = [view(UR, i * KB, [512], BF16) for i in range(2)]
        zs = [view(UR, (2 + i) * KB, [512], BF16) for i in range(2)]
        mean_sb = view(UR, 4 * KB, [512], F32)
        var_sb = view(UR, 6 * KB, [512], F32)
        sd_sb = view(UR, 8 * KB, [512], F32)
        tbuf = [view(UR, (10 + 2 * i) * KB, [512], F32) for i in range(2)]
        Pr = [view(UR, (14 + i) * KB, [512], BF16) for i in range(4)]
        rs_sb = view(UR, 18 * KB, [512], F32)
        ln_cnt = [0]

        def layernorm(z, w, gname, l):
            pm = nbank()
            pe2 = nbank()
            for m in range(8):
                i = ln_cnt[0] % 2
                ln_cnt[0] += 1
                act(zb[i][:, :w], z[:, m, :w], AF.Copy)
                act(zs[i][:, :w], z[:, m, :w], AF.Square)
                mm(pm[:, :w], ones["1024"][:], zb[i][:, :w], start=(m == 0), stop=(m == 7))
                mm(pe2[:, :w], ones["1024"][:], zs[i][:, :w], start=(m == 0), stop=(m == 7))
            act(mean_sb[:, :w], pm[:, :w], AF.Copy)
            stt(var_sb[:, :w], mean_sb[:, :w], -1.0, mean_sb[:, :w], ALU.mult, ALU.mult)
            stt(var_sb[:, :w], pe2[:, :w], LN_EPS, var_sb[:, :w], ALU.add, ALU.add)
            recip(sd_sb[:, :w], var_sb[:, :w], -0.5)
            for m in range(8):
                t_ = tbuf[m % 2]
                tt(t_[:, :w], z[:, m, :w], mean_sb[:, :w], ALU.subtract)
                tt(t_[:, :w], t_[:, :w], sd_sb[:, :w], ALU.mult)
                act(z[:, m, :w], t_[:, :w], AF.Identity, scale=vcol(gname + "_g", l * 8 + m),
                    bias=vcol(gname + "_b", l * 8 + m))

        def rms_stats(sq_list, w, ones_t):
            pb = nbank()
            n = len(sq_list)
            for i, sq in enumerate(sq_list):
                k = sq.shape[0]
                mm(pb[:, :w], ones_t[0:k, :], sq, start=(i == 0), stop=(i == n - 1))
            ts(var_sb[:, :w], pb[:, :w], RMS_EPS, None, ALU.add)
            recip(sd_sb[:, :w], var_sb[:, :w], -0.5)

        accDs = [view(UR, 4 * KB, [512], F32), view(UR, 0 * KB, [512], F32)]
        accPs = [view(UR, 6 * KB, [512], F32), view(UR, 2 * KB, [512], F32)]
        accb = view(UR, 8 * KB, [512], BF16)

        def attention(qparts, kparts, vfn, kts, qcols, ocb, scale):
            nk = len(kts)
            steps = [(qi, qc, ki, kt) for qi, qc in enumerate(qcols) for ki, kt in enumerate(kts)]
            n = len(steps)
            sb = [PS[0], PS[1], PS[2]]
            ob = [PS[3], PS[4]]
            smb = PS[5]
            deferred = []

            def pv(si):
                qi, (c0, w), ki, kt = steps[si]
                P = Pr[si % 4]
                accD, accP = accDs[qi % 2], accPs[qi % 2]
                mm(ob[qi % 2][:, :w], vfn(kt), P[:, :w], start=(ki == 0), stop=(ki == nk - 1))
                if ki == nk - 1:
                    def fin(w=w, accD=accD, accP=accP, c0=c0, qi=qi):
                        tt(accb[:, :w], accD[:, :w], accP[:, :w], ALU.add)
                        mm(smb[:, :w], ones["1"][:], accb[:, :w])
                        recip(rs_sb[:, :w], smb[:, :w])
                        ocb(c0, w, ob[qi % 2], rs_sb)
                    deferred.append((si + 5, fin))

            for si in range(n):
                qi, (c0, w), ki, kt = steps[si]
                accD, accP = accDs[qi % 2], accPs[qi % 2]
                sp_ = sb[si % 3]
                for i, (qp, kp) in enumerate(zip(qparts, kparts)):
                    mm(sp_[:, :w], kp[:, kt * 128:(kt + 1) * 128], qp[:, c0:c0 + w],
                       start=(i == 0), stop=(i == len(qparts) - 1))
                P = Pr[si % 4]
                act(P[:, :w], sp_[:, :w], AF.Exp, scale=scale)
                acc_ = accD if ki % 2 == 0 else accP
                if ki < 2:
                    S.add("dve", lambda e, P=P, w=w, acc_=acc_: e.tensor_copy(out=acc_[:, :w], in_=P[:, :w]),
                          reads=[P[:, :w]], writes=[acc_[:, :w]])
                else:
                    tt(acc_[:, :w], acc_[:, :w], P[:, :w], ALU.add)
                if si >= 2:
                    pv(si - 2)
                while deferred and deferred[0][0] <= si:
                    deferred.pop(0)[1]()
            for si in range(max(0, n - 2), n):
                pv(si)
            while deferred:
                deferred.pop(0)[1]()

        def phase_c(l, bi, w_o_dram, OT, need_ctx):
            wo = [wload(rows(w_o_dram, 0, 8, h * 512, 512), [8, 512]) for h in range(2)]
            Z = [view(HR, i * 16 * KB, [8, 512], F32) for i in range(2)] + \
                [view(XR, (36 + 16 * i) * KB, [8, 512], F32) for i in range(2)]
            LSEM = [11, 12, 21, 22]
            SSEM = [23, 18]
            chunks = [c for c in TCH if (need_ctx or not c[2])]
            nch = len(chunks)

            def load(ci):
                c0, w, isctx = chunks[ci]
                z = Z[ci % 4]
                dma("sp", z[:, :, :w], xpark[:, :, c0:c0 + w], LSEM[ci % 4],
                    reads=[("xpark", c0)], writes=[z[:, :, :w]])

            for ci in range(min(2, nch)):
                load(ci)
            prev = None
            for ci, (c0, w, isctx) in enumerate(chunks):
                z = Z[ci % 4]
                if ci + 2 < nch:
                    load(ci + 2)
                col = 2 if isctx else bi
                for m in range(8):
                    pb = nbank()
                    for k in range(8):
                        mm(pb[:, :w], wo[m // 4][:, k, (m % 4) * 128:(m % 4 + 1) * 128], OT[:, k, c0:c0 + w],
                           start=(k == 0), stop=(k == 7))
                    stt(z[:, m, :w], pb[:, :w], mod(l, 16 + m, col), z[:, m, :w], ALU.mult, ALU.add)
                if prev is not None:
                    prev()
                def fin(z=z, w=w, c0=c0, ci=ci):
                    layernorm(z, w, "ln1", l)
                    dma("sp", xpark[:, :, c0:c0 + w], z[:, :, :w], SSEM[ci % 2],
                        reads=[z[:, :, :w]], writes=[("xpark", c0)])
                prev = fin
            prev()

        def mlp(l, bi, need_ctx, x_in_dram, hook=None):
            chunks = [c for c in TCH if (need_ctx or not c[2])]
            if x_in_dram:
                for ci, (c0, w, isctx) in enumerate(chunks):
                    dma("sp", xT[:, :, c0:c0 + w], xpark[:, :, c0:c0 + w], SEM_XIN[ci % 2],
                        reads=[("xpark", c0)], writes=[xT[:, :, c0:c0 + w]])
            modulate(l, bi, 1, xT, hT, chunks)
            hid = [view(TR, i * 4 * KB, [4, 512], BF16) for i in range(3)]
            rl = [view(TR, (12 + i) * KB, [512], BF16) for i in range(2)]
            nch = len(chunks)
            wts = {}

            def getw(j):
                if j not in wts:
                    wts[j] = (wload(rows(mlp_w1[l], 0, 8, j * 512, 512), [8, 512]),
                              wload(rows(mlp_w2[l], j * 512, 4, 0, 1024), [4, 1024]))
                return wts[j]

            items = [(j, ci) for j in range(6) for ci in range(nch)] + \
                    [(j, ci) for ci in range(nch) for j in (6, 7)]
            cnt = 0
            ln_q = []
            pend = None
            for (j, ci) in items:
                c0, w, isctx = chunks[ci]
                w1, w2 = getw(j)
                if ci == 0 and j + 1 < 8:
                    getw(j + 1)
                hd = hid[cnt % 3]
                cnt += 1
                for hc in range(4):
                    pb = nbank()
                    for k in range(8):
                        mm(pb[:, :w], w1[:, k, hc * 128:(hc + 1) * 128], hT[:, k, c0:c0 + w],
                           start=(k == 0), stop=(k == 7))
                    r_ = rl[hc % 2]
                    act(r_[:, :w], pb[:, :w], AF.Relu)
                    tt(hd[:, hc, :w], pb[:, :w], r_[:, :w], ALU.mult)
                if pend is not None:
                    pend()

                def second(hd=hd, c0=c0, w=w, isctx=isctx, j=j, w2=w2):
                    col = 2 if isctx else bi
                    for m in range(8):
                        pb = nbank()
                        for k in range(4):
                            mm(pb[:, :w], w2[:, k, m * 128:(m + 1) * 128], hd[:, k, :w],
                               start=(k == 0), stop=(k == 3))
                        stt(xT[:, m, c0:c0 + w], pb[:, :w], mod(l, 40 + m, col), xT[:, m, c0:c0 + w],
                            ALU.mult, ALU.add)
                    if j == 7:
                        ln_q.append((c0, w))
                pend = second
                if len(ln_q) >= 2:
                    c0_, w_ = ln_q.pop(0)
                    layernorm(xT[:, :, c0_:c0_ + w_], w_, "ln2", l)
                    if hook is not None:
                        hook(c0_, w_, c0_ == L)
            pend()
            while ln_q:
                c0_, w_ = ln_q.pop(0)
                layernorm(xT[:, :, c0_:c0_ + w_], w_, "ln2", l)
                if hook is not None:
                    hook(c0_, w_, c0_ == L)

        def load_tables(src):
            cos = view(TR, 0, [L], F32)
            sin = view(TR, 8 * KB, [L], F32)
            dma("sp", cos, src[0], SEM_TAB)
            dma("sp", sin, src[1], SEM_TAB)
            return cos, sin

        def park_chunk(c0, w):
            dma("sp", xpark[:, :, c0:c0 + w], xT[:, :, c0:c0 + w], SEM_PARK,
                reads=[xT[:, :, c0:c0 + w]], writes=[("xpark", c0)])

        def park():
            for (c0, w, isctx) in TCH:
                park_chunk(c0, w)

        def phase_a_chunk(l2, bi, c0, w, isctx):
            modulate(l2, bi, 0, xT, hT, [(c0, w, isctx)])
            park_chunk(c0, w)

        def mixer_gqa(l, bi, need_ctx, pre=False):
            j = l // 3
            cos, sin = load_tables(rope_g)
            if not pre:
                modulate(l, bi, 0, xT, hT)
                park()
            OT = view(XR, 0, [8, T], BF16)
            KT = view(XR, 36 * KB, [2, T], BF16)
            VT = view(XR, 45 * KB, [18, 256], BF16)
            QH = [view(XR, (54 + 5 * i) * KB, [T], BF16) for i in range(2)]
            sq = view(UR, 0, [512], BF16)
            ta = view(UR, 10 * KB, [512], F32)
            tb_ = view(UR, 12 * KB, [512], F32)
            scale = 128.0 ** -0.5
            wk = wload(rows(gqa_w_qkv[0], 0, 8, 1024, 256), [8, 256])
            wkP = wload(rows(gqa_w_qkvP[0], 0, 8, 1024, 256), [8, 256])
            wv = wload(rows(gqa_w_qkv[0], 0, 8, 1280, 256), [8, 256])

            def qk_head(w_, wP_, m0, dst, gname, chunks):
                for (c0, w, isctx) in chunks:
                    pq = nbank()
                    for k in range(8):
                        mm(pq[:, :w], w_[:, k, m0:m0 + 128], hT[:, k, c0:c0 + w], start=(k == 0), stop=(k == 7))
                    act(sq[:, :w], pq[:, :w], AF.Square)
                    rms_stats([sq[:, :w]], w, ones["128"])
                    if isctx:
                        stt(dst[:, c0:c0 + w], pq[:, :w], vcol(gname, 0), sd_sb[:, :w], ALU.mult, ALU.mult)
                    else:
                        pp = nbank()
                        for k in range(8):
                            mm(pp[:, :w], wP_[:, k, m0:m0 + 128], hT[:, k, c0:c0 + w], start=(k == 0), stop=(k == 7))
                        stt(ta[:, :w], pq[:, :w], vcol(gname, 0), cos[:, c0:c0 + w], ALU.mult, ALU.mult)
                        stt(tb_[:, :w], pp[:, :w], vcol(gname + "P", 0), sin[:, c0:c0 + w], ALU.mult, ALU.mult)
                        tt(ta[:, :w], ta[:, :w], tb_[:, :w], ALU.add)
                        tt(dst[:, c0:c0 + w], ta[:, :w], sd_sb[:, :w], ALU.mult)

            for kh in range(2):
                qk_head(wk, wkP, kh * 128, KT[:, kh, :], "gqa_kn", TCH)
            for t in range(18):
                if t % 2 == 0:
                    pb = nbank()
                o = pb[:, (t % 2) * 256:(t % 2 + 1) * 256]
                for k in range(8):
                    mm(o, hT[:, k, t * 128:(t + 1) * 128], wv[:, k, :], start=(k == 0), stop=(k == 7))
                if t % 2 == 1:
                    copy(VT[:, t - 1:t + 1, :], pb[:, :].rearrange("p (a b) -> p a b", a=2), (t // 2) % 2)
            qchunks = [c for c in TCH if (need_ctx or not c[2])]
            for h in range(8):
                sl = wslot()
                wq = wload(rows(gqa_w_qkv[0], 0, 8, h * 128, 128), [8, 128], slot=sl, off=0)
                wqP = wload(rows(gqa_w_qkvP[0], 0, 8, h * 128, 128), [8, 128], slot=sl, off=2 * KB)
                qh = QH[h % 2]
                qk_head(wq, wqP, 0, qh, "gqa_qn", qchunks)
                kh = h // 4

                def ocb(c0, w, o_ps, rs, h=h):
                    tt(OT[:, h, c0:c0 + w], o_ps[:, :w], rs[:, :w], ALU.mult)

                attention([qh], [KT[:, kh, :]], lambda kt, kh=kh: VT[:, kt, kh * 128:(kh + 1) * 128],
                          list(range(18)), [(c0, w) for (c0, w, ic) in TCH if not ic], ocb, scale)
                if need_ctx:
                    attention([qh], [KT[:, kh, :]], lambda kt, kh=kh: VT[:, kt, kh * 128:(kh + 1) * 128],
                              [16, 17], [(L, C)], ocb, scale)
            phase_c(l, bi, gqa_w_o[0], OT, need_ctx)

        def mixer_mla(l, bi, need_ctx, pre=False):
            j = l // 3
            cos, sin = load_tables(rope_m)
            if not pre:
                modulate(l, bi, 0, xT, hT)
                park()
            OT = view(XR, 0, [8, T], BF16)
            CQ = view(XR, 36 * KB, [4, T], BF16)
            CKV = view(XR, 54 * KB, [2, T], BF16)
            KPE = view(XR, 63 * KB, [T], BF16)
            sqr = [view(UR, i * KB, [512], BF16) for i in range(4)]
            ta = view(UR, 10 * KB, [512], F32)
            tb_ = view(UR, 12 * KB, [512], F32)
            scale = 192.0 ** -0.5
            wdq = wload(rows(mla_w_dq[j], 0, 8, 0, 512), [8, 512])
            sl = wslot()
            wdkv = wload(rows(mla_w_dkv[j], 0, 8, 0, 320), [8, 320], slot=sl, off=0)
            wdkvP = wload(rows(mla_w_dkvP[j], 0, 8, 0, 128), [8, 128], slot=sl, off=5 * KB)
            S.add("dve", lambda e: e.memset(KPE[0:64, :], 0.0), writes=[KPE[0:64, :]])
            qchunks = [c for c in TCH if (need_ctx or not c[2])]
            for (c0, w, isctx) in TCH:
                if need_ctx or not isctx:
                    pbs = []
                    for m in range(4):
                        pb = nbank()
                        pbs.append(pb)
                        for k in range(8):
                            mm(pb[:, :w], wdq[:, k, m * 128:(m + 1) * 128], hT[:, k, c0:c0 + w],
                               start=(k == 0), stop=(k == 7))
                        act(sqr[m][:, :w], pb[:, :w], AF.Square)
                    rms_stats([sqr[m][:, :w] for m in range(4)], w, ones["512"])
                    for m in range(4):
                        stt(CQ[:, m, c0:c0 + w], pbs[m][:, :w], vcol("mla_qn", j * 4 + m), sd_sb[:, :w],
                            ALU.mult, ALU.mult)
                pbs = []
                for m in range(2):
                    pb = nbank()
                    pbs.append(pb)
                    for k in range(8):
                        mm(pb[:, :w], wdkv[:, k, m * 128:(m + 1) * 128], hT[:, k, c0:c0 + w],
                           start=(k == 0), stop=(k == 7))
                    act(sqr[m][:, :w], pb[:, :w], AF.Square)
                rms_stats([sqr[m][:, :w] for m in range(2)], w, ones["256"])
                for m in range(2):
                    stt(CKV[:, m, c0:c0 + w], pbs[m][:, :w], vcol("mla_kvn", j * 2 + m), sd_sb[:, :w],
                        ALU.mult, ALU.mult)
                pk = nbank()
                for k in range(8):
                    mm(pk[:, :w], wdkv[:, k, 192:320], hT[:, k, c0:c0 + w], start=(k == 0), stop=(k == 7))
                if isctx:
                    copy(KPE[64:128, c0:c0 + w], pk[64:128, :w], 1)
                else:
                    pp = nbank()
                    for k in range(8):
                        mm(pp[:, :w], wdkvP[:, k, :], hT[:, k, c0:c0 + w], start=(k == 0), stop=(k == 7))
                    tt(ta[64:128, :w], pk[64:128, :w], cos[64:128, c0:c0 + w], ALU.mult)
                    tt(tb_[64:128, :w], pp[64:128, :w], sin[64:128, c0:c0 + w], ALU.mult)
                    tt(KPE[64:128, c0:c0 + w], ta[64:128, :w], tb_[64:128, :w], ALU.add)
            wuq = [wload(rows(mla_w_uq[j], 0, 4, g * 768, 768), [4, 768]) for g in range(2)]
            wuqP = wload(rows(mla_w_uqP[j], 0, 4, 0, 1024), [4, 1024])
            sl = wslot()
            for kk in range(2):
                wload(mla_w_ukv[j][kk * 128:(kk + 1) * 128, :], [2048], slot=sl, off=kk * 4 * KB)
            wukv = wview(sl, [2, 2048])
            hb = []
            for i in range(2):
                o = i * 18 * KB
                hb.append(dict(qn=view(HR, o, [T], BF16), qpe=view(HR, o + 4608, [T], BF16),
                               kn=view(HR, o + 9216, [T], BF16), v=view(HR, o + 13824, [18, 128], BF16)))
                qz = hb[-1]["qpe"]
                S.add("dve", lambda e, qz=qz: e.memset(qz[0:64, :], 0.0), writes=[qz[0:64, :]])
            for h in range(8):
                B = hb[h % 2]
                wq = wuq[h // 4]
                hq = (h % 4) * 192
                for (c0, w, isctx) in qchunks:
                    pb = nbank((6, 7))
                    for k in range(4):
                        mm(pb[:, :w], wq[:, k, hq:hq + 128], CQ[:, k, c0:c0 + w], start=(k == 0), stop=(k == 3))
                    copy(B["qn"][:, c0:c0 + w], pb[:, :w], 0)
                    pk = nbank((6, 7))
                    for k in range(4):
                        mm(pk[:, :w], wq[:, k, hq + 64:hq + 192], CQ[:, k, c0:c0 + w], start=(k == 0), stop=(k == 3))
                    if isctx:
                        copy(B["qpe"][64:128, c0:c0 + w], pk[64:128, :w], 1)
                    else:
                        pp = nbank((6, 7))
                        for k in range(4):
                            mm(pp[:, :w], wuqP[:, k, h * 128:(h + 1) * 128], CQ[:, k, c0:c0 + w],
                               start=(k == 0), stop=(k == 3))
                        tt(ta[64:128, :w], pk[64:128, :w], cos[64:128, c0:c0 + w], ALU.mult)
                        tt(tb_[64:128, :w], pp[64:128, :w], sin[64:128, c0:c0 + w], ALU.mult)
                        tt(B["qpe"][64:128, c0:c0 + w], ta[64:128, :w], tb_[64:128, :w], ALU.add)
                for (c0, w, isctx) in TCH:
                    pb = nbank((6, 7))
                    for k in range(2):
                        mm(pb[:, :w], wukv[:, k, h * 256:h * 256 + 128], CKV[:, k, c0:c0 + w],
                           start=(k == 0), stop=(k == 1))
                    copy(B["kn"][:, c0:c0 + w], pb[:, :w], 0)
                for t in range(18):
                    if t % 4 == 0:
                        pb = nbank((6, 7))
                        n4 = min(4, 18 - t)
                    o = pb[:, (t % 4) * 128:(t % 4 + 1) * 128]
                    for k in range(2):
                        mm(o, CKV[:, k, t * 128:(t + 1) * 128], wukv[:, k, h * 256 + 128:h * 256 + 256],
                           start=(k == 0), stop=(k == 1))
                    if t % 4 == n4 - 1:
                        t0 = t - (n4 - 1)
                        copy(B["v"][:, t0:t0 + n4, :], pb[:, :n4 * 128].rearrange("p (a b) -> p a b", a=n4), 1)

                def ocb(c0, w, o_ps, rs, h=h):
                    tt(OT[:, h, c0:c0 + w], o_ps[:, :w], rs[:, :w], ALU.mult)

                attention([B["qn"], B["qpe"]], [B["kn"], KPE], lambda kt, B=B: B["v"][:, kt, :],
                          list(range(18)), [(c0, w) for (c0, w, ic) in TCH if not ic], ocb, scale)
                if need_ctx:
                    attention([B["qn"], B["qpe"]], [B["kn"], KPE], lambda kt, B=B: B["v"][:, kt, :],
                              [16, 17], [(L, C)], ocb, scale)
            phase_c(l, bi, mla_w_o[j], OT, need_ctx)

        def mixer_na(l, bi, need_ctx, pre=False):
            if not pre:
                modulate(l, bi, 0, xT, hT)
                park()
            OT = view(XR, 0, [8, T], BF16)
            cb = []
            for i in range(2):
                o = (36 + 18 * i) * KB
                cb.append(dict(qA=view(XR, o, [T], BF16), qB=view(XR, o + 4608, [T], BF16),
                               k=view(XR, o + 9216, [T], BF16), v=view(XR, o + 13824, [18, 128], BF16)))
                qa, qb_ = cb[-1]["qA"], cb[-1]["qB"]
                S.add("dve", lambda e, qa=qa: e.memset(qa[64:128, :], 0.0), writes=[qa[64:128, :]])
                S.add("dve", lambda e, qb_=qb_: e.memset(qb_[0:64, :], 0.0), writes=[qb_[0:64, :]])
            otmp = view(UR, 10 * KB, [512], F32)
            maskt = view(TR, 0, [NPAT * 128], BF16)
            dma("pool", maskt, na_mask[:, :], 19, max_dma_last_dim=4096)
            Et = [view(TR, (5376 * (1 + i)), [NPAT * 128], BF16) for i in range(2)]
            Pn = [view(UR, (14 + 2 * i) * KB, [896], BF16) for i in range(2)]
            scale = 64.0 ** -0.5
            pbase = {}
            for t in range(16):
                pbase[t] = NA_PER_T[t][0][1]
            for m in range(8):
                B = cb[m % 2]
                sl = wslot()
                wq = wload(rows(na_w_qkv[0], 0, 8, m * 128, 128), [8, 128], slot=sl, off=0)
                wk = wload(rows(na_w_qkv[0], 0, 8, 1024 + m * 128, 128), [8, 128], slot=sl, off=2 * KB)
                wv = wload(rows(na_w_qkv[0], 0, 8, 2048 + m * 128, 128), [8, 128], slot=sl, off=4 * KB)
                for (c0, w, isctx) in TCH:
                    pb = nbank((6, 7))
                    for k in range(8):
                        mm(pb[:, :w], wq[:, k, :], hT[:, k, c0:c0 + w], start=(k == 0), stop=(k == 7))
                    act(B["qA"][0:64, c0:c0 + w], pb[0:64, :w], AF.Identity, bias=vcol("na_b", m)[0:64, :])
                    act(B["qB"][64:128, c0:c0 + w], pb[64:128, :w], AF.Identity, bias=vcol("na_b", m)[64:128, :])
                    pb = nbank((6, 7))
                    for k in range(8):
                        mm(pb[:, :w], wk[:, k, :], hT[:, k, c0:c0 + w], start=(k == 0), stop=(k == 7))
                    ts(B["k"][:, c0:c0 + w], pb[:, :w], vcol("na_b", 8 + m), None, ALU.add)
                for t in range(18):
                    if t % 4 == 0:
                        pb = nbank((6, 7))
                        n4 = min(4, 18 - t)
                    o = pb[:, (t % 4) * 128:(t % 4 + 1) * 128]
                    for k in range(8):
                        mm(o, hT[:, k, t * 128:(t + 1) * 128], wv[:, k, :], start=(k == 0), stop=(k == 7))
                    if t % 4 == n4 - 1:
                        t0 = t - (n4 - 1)
                        copy(B["v"][:, t0:t0 + n4, :], pb[:, :n4 * 128].rearrange("p (a b) -> p a b", a=n4), 1)
                for e_ in range(2):
                    hd = 2 * m + e_
                    E = Et[hd % 2]
                    dma("pool", E, na_bias[hd], SEM_E[hd % 2], max_dma_last_dim=4096)
                    act(E, E, AF.Exp)
                    tt(E, E, maskt, ALU.mult)
                    p0, p1 = e_ * 64, (e_ + 1) * 64
                    Bq = B["qA"] if e_ == 0 else B["qB"]
                    def na_pv(t, tiles, P):
                        nt = len(tiles)
                        g = t // 4
                        oc = (t % 4) * 128
                        for i, kt in enumerate(tiles):
                            mm(ob[g % 2][:, oc:oc + 128], B["v"][:, kt, :], P[:, i * 128:(i + 1) * 128],
                               start=(i == 0), stop=(i == nt - 1))
                        for i, kt in enumerate(tiles):
                            mm(smb[g % 2][:, oc:oc + 128], ones["1"][:], P[:, i * 128:(i + 1) * 128],
                               start=(i == 0), stop=(i == nt - 1))
                        if t % 4 == 3:
                            recip(rs_sb[p0:p1, :], smb[g % 2][p0:p1, :])
                            tt(otmp[p0:p1, :], ob[g % 2][p0:p1, :], rs_sb[p0:p1, :], ALU.mult)
                            ts(OT[p0:p1, m, g * 512:(g + 1) * 512], otmp[p0:p1, :], vcol("na_b", 16 + m)[p0:p1, :], None,
                               ALU.add)

                    ob = [PS[4], PS[5]]
                    smb = [PS[6], PS[7]]
                    pend_na = None
                    for t in range(16):
                        lst = NA_PER_T[t]
                        nl = len(lst)
                        tiles = [kt for kt, _ in lst] + [16, 17]
                        nt = len(tiles)
                        P = Pn[t % 2]
                        sbk = [PS[0], PS[1]] if t % 2 == 0 else [PS[2], PS[3]]
                        for i, kt in enumerate(tiles):
                            mm(sbk[i // 4][:, (i % 4) * 128:(i % 4 + 1) * 128], B["k"][:, kt * 128:(kt + 1) * 128],
                               Bq[:, t * 128:(t + 1) * 128])
                        act(P[:, 0:512], sbk[0][:, :], AF.Exp, scale=scale)
                        act(P[:, 512:nt * 128], sbk[1][:, 0:(nt - 4) * 128], AF.Exp, scale=scale)
                        eb = pbase[t] * 128
                        tt(P[:, 0:nl * 128], P[:, 0:nl * 128], E[:, eb:eb + nl * 128], ALU.mult)
                        if pend_na is not None:
                            pend_na()
                        pend_na = (lambda t=t, tiles=tiles, P=P: na_pv(t, tiles, P))
                    if pend_na is not None:
                        pend_na()
                    if need_ctx:
                        P = Pn[0]
                        for i, kt in enumerate((16, 17)):
                            mm(PS[0][:, i * 256:(i + 1) * 256], B["k"][:, kt * 128:(kt + 1) * 128],
                               Bq[:, L:L + C])
                        act(P[:, 0:512], PS[0][:, :], AF.Exp, scale=scale)
                        for i, kt in enumerate((16, 17)):
                            mm(ob[0][:, 0:256], B["v"][:, kt, :], P[:, i * 256:(i + 1) * 256],
                               start=(i == 0), stop=(i == 1))
                        for i, kt in enumerate((16, 17)):
                            mm(smb[0][:, 0:256], ones["1"][:], P[:, i * 256:(i + 1) * 256],
                               start=(i == 0), stop=(i == 1))
                        recip(rs_sb[p0:p1, 0:256], smb[0][p0:p1, 0:256])
                        tt(otmp[p0:p1, 0:256], ob[0][p0:p1, 0:256], rs_sb[p0:p1, 0:256], ALU.mult)
                        ts(OT[p0:p1, m, L:L + C], otmp[p0:p1, 0:256], vcol("na_b", 16 + m)[p0:p1, :], None, ALU.add)
            phase_c(l, bi, na_w_o[0], OT, need_ctx)

        xs = [view(HR, i * 4 * KB, [1024], F32) for i in range(3)]
        ost = [view(HR, (16 + i * 4) * KB, [1024], F32) for i in range(2)]
        for bi in range(nb):
            for t in range(18):
                r = t % 3
                src = x2[bi, t * 128:(t + 1) * 128, :] if t < 16 else ctx2[bi, (t - 16) * 128:(t - 15) * 128, :]
                dma("sp", xs[r], src, SEM_STG[r])
                for half in range(2):
                    pb = nbank()
                    for c in range(4):
                        cc = half * 4 + c
                        tp(pb[:, c * 128:(c + 1) * 128], xs[r][:, cc * 128:(cc + 1) * 128], ident[:])
                    copy(xT[:, half * 4:(half + 1) * 4, t * 128:(t + 1) * 128],
                         pb[:, :].rearrange("p (a b) -> p a b", a=4), half)
            for li, l in enumerate(layers):
                kind = l % 3
                need_ctx = l < DEPTH - 1
                pre = li > 0 and do_mixer and do_mlp
                if do_mixer:
                    if kind == 0:
                        mixer_mla(l, bi, need_ctx, pre)
                    elif kind == 1:
                        mixer_na(l, bi, need_ctx, pre)
                    else:
                        mixer_gqa(l, bi, need_ctx, pre)
                if do_mlp:
                    hook = None
                    if do_mixer and li + 1 < len(layers):
                        l2 = layers[li + 1]
                        hook = (lambda c0, w, ic, l2=l2, bi=bi: phase_a_chunk(l2, bi, c0, w, ic))
                    mlp(l, bi, need_ctx, x_in_dram=do_mixer, hook=hook)
                elif do_mixer:
                    for ci, (c0, w, isctx) in enumerate(TCH):
                        if need_ctx or not isctx:
                            dma("sp", xT[:, :, c0:c0 + w], xpark[:, :, c0:c0 + w], SEM_XIN[ci % 2],
                                reads=[("xpark", c0)], writes=[xT[:, :, c0:c0 + w]])
            for t in range(16):
                o_ = ost[t % 2]
                for half in range(2):
                    pb = nbank()
                    for c in range(4):
                        cc = half * 4 + c
                        tp(pb[:, c * 128:(c + 1) * 128], xT[:, cc, t * 128:(t + 1) * 128], ident[:])
                    copy(o_[:, half * 512:(half + 1) * 512], pb[:, :], half)
                dma("sp", y2[bi, t * 128:(t + 1) * 128, :], o_, SEM_OUT[t % 2], reads=[o_], writes=[("y2", bi, t)])
        S.add("sp", None, reads=[("y2", b_, t) for b_ in range(nb) for t in range(16)])
        S.emit(block, sems, dma_sems)
    return nc


def make_in_maps(inp, ncores=8):
    f = lambda k: np.ascontiguousarray(np.asarray(inp[k], np.float32))
    p64 = _rope_perm(64)
    p128 = _rope_perm(128)
    w_uq = f("mla_w_uq")
    pe_cols = w_uq.reshape(2, 512, 8, 192)[:, :, :, 128:]
    w_uqP = np.ascontiguousarray(np.concatenate([pe_cols, pe_cols[:, :, :, p64]], axis=3).reshape(2, 512, 1024))
    w_dkv = f("mla_w_dkv")
    w_dkvP = np.ascontiguousarray(np.concatenate([w_dkv[:, :, 256:], w_dkv[:, :, 256:][:, :, p64]], axis=2))
    gw = f("gqa_w_qkv")
    gwP = np.ascontiguousarray(gw[:, :, :1280].reshape(1, D, 10, 128)[:, :, :, p128].reshape(1, D, 1280))
    rpb = f("na_rpb")[0]
    nbias = rpb[:, NA_DR, NA_DC]
    nbias = np.ascontiguousarray(nbias.transpose(0, 2, 1, 3).reshape(16, 128, NPAT * 128))
    nmask = np.ascontiguousarray(NA_VALID.transpose(1, 0, 2).reshape(128, NPAT * 128))
    bq = f("na_b_qkv")
    qn = f("gqa_q_norm")[0]
    kn = f("gqa_k_norm")[0]
    vec = np.concatenate([
        _fm(f("ada_b")), _fm(f("ln1_g")), _fm(f("ln1_b")), _fm(f("ln2_g")), _fm(f("ln2_b")),
        _fm(f("mla_q_norm")), _fm(f("mla_kv_norm")), _fm(bq[0]),
        _fm(qn), _fm(qn[p128]), _fm(kn), _fm(kn[p128])], axis=1)
    assert vec.shape == (128, NV), vec.shape
    shared = dict(
        vecs=np.ascontiguousarray(vec), ident=np.eye(128, dtype=np.float32),
        ada_w=f("ada_w"), mlp_w1=f("mlp_w1"), mlp_w2=f("mlp_w2"),
        mla_w_dq=f("mla_w_dq"), mla_w_uq=w_uq, mla_w_uqP=w_uqP, mla_w_dkv=w_dkv, mla_w_dkvP=w_dkvP,
        mla_w_ukv=f("mla_w_ukv"), mla_w_o=f("mla_w_o"),
        na_w_qkv=f("na_w_qkv"), na_w_o=f("na_w_o"), na_bv=np.ascontiguousarray(bq[:, 2048:]),
        na_bias=nbias, na_mask=nmask,
        gqa_w_qkv=gw, gqa_w_qkvP=gwP, gqa_w_o=f("gqa_w_o"),
        rope_m=_rope_tables(64), rope_g=_rope_tables(128),
    )
    x = f("x")
    ctx = f("ctx")
    c = f("c")
    cc = f("c_ctx")
    maps = []
    for i in range(ncores):
        c3 = np.stack([c[2 * i], c[2 * i + 1], cc], 0)
        cTm = np.ascontiguousarray(c3.reshape(3, 8, 128).transpose(2, 1, 0).reshape(128, 24))
        m = dict(shared)
        m["x2"] = np.ascontiguousarray(x[2 * i:2 * i + 2])
        m["ctx2"] = np.ascontiguousarray(ctx[2 * i:2 * i + 2])
        m["cT"] = cTm
        maps.append(m)
    return maps


_NC = None


def kernel(**inputs):
    global _NC
    if _NC is None:
        _NC = build()
    maps = make_in_maps(inputs, 8)
    res = run_bass_kernel_spmd(_NC, maps, core_ids=list(range(8)))
    out = np.concatenate([np.asarray(r["y2"], np.float32) for r in res.results], axis=0)
    return out
```
